# Optimizing a Trainium2 kernel written in Bass

```python
import math
import jax, jax.numpy as jnp
from jax import lax
import numpy as np

D_MODEL = 4096
BATCH = 1
SEQ = 16384
DEPTH = 2

D_FF = 11008
ALPHA = (2 * DEPTH) ** 0.25
BETA = (8 * DEPTH) ** -0.25
LN_EPS = 1e-5
RMS_EPS = 1e-6
Q_BLOCK = 128
FOX_HEADS = 24
FOX_HEAD_DIM = 128
FOX_WIDTH = FOX_HEADS * FOX_HEAD_DIM
POOL_WINDOWS = (2, 4, 8, 16)
POOL_GROUPS = 4
POOL_GROUP_DIM = 256
POOL_WIDTH = POOL_GROUPS * POOL_GROUP_DIM
IN0_WIDTH = 3 * FOX_WIDTH + FOX_HEADS + POOL_WIDTH
MIX0_WIDTH = FOX_WIDTH + POOL_WIDTH
CONV_WIDTH = 1024
CONV_TAPS = 31
MLA_HEADS = 24
MLA_Q_RANK = 1024
MLA_KV_RANK = 512
MLA_NOPE_DIM = 128
MLA_ROPE_DIM = 64
MLA_V_DIM = 128
ROPE_BASE = 10000.0
IN1_WIDTH = 2 * CONV_WIDTH + MLA_Q_RANK + MLA_KV_RANK + MLA_ROPE_DIM
MIX1_WIDTH = CONV_WIDTH + MLA_HEADS * MLA_V_DIM

kernel_name = "fox_pool_conformer_mla_deepnorm_hybrid"


def layer_norm(x, g, b):
    xf = x.astype(jnp.float32)
    mu = jnp.mean(xf, axis=-1, keepdims=True)
    var = jnp.mean(jnp.square(xf - mu), axis=-1, keepdims=True)
    return ((xf - mu) * lax.rsqrt(var + LN_EPS) * g + b).astype(x.dtype)


def rms_norm(x, g):
    xf = x.astype(jnp.float32)
    return (xf * lax.rsqrt(jnp.mean(jnp.square(xf), axis=-1, keepdims=True) + RMS_EPS) * g).astype(x.dtype)


def swiglu(x, w1, w3, w2):
    return (jax.nn.silu(x @ w1) * (x @ w3)) @ w2


def causal_block_attention(q, k, v, c=None):
    B, S, H, Dk = q.shape
    Dv = v.shape[-1]
    nb = S // Q_BLOCK
    scale = Dk ** -0.5
    key_pos = jnp.arange(S)
    starts = jnp.arange(nb) * Q_BLOCK
    qb = q.reshape(B, nb, Q_BLOCK, H, Dk).transpose(1, 0, 2, 3, 4)
    ct = None if c is None else c.transpose(0, 2, 1)

    def block(qi, start, ci):
        s = jnp.einsum('bqhd,bkhd->bhqk', qi, k).astype(jnp.float32) * scale
        if ci is not None:
            s = s + ci[..., None] - ct[:, :, None, :]
        qpos = start + jnp.arange(Q_BLOCK)
        mask = qpos[:, None] >= key_pos[None, :]
        s = jnp.where(mask, s, -jnp.inf)
        p = jax.nn.softmax(s, axis=-1).astype(v.dtype)
        return jnp.einsum('bhqk,bkhd->bqhd', p, v)

    if c is None:
        out = lax.map(lambda a: block(a[0], a[1], None), (qb, starts))
    else:
        cb = ct.reshape(B, H, nb, Q_BLOCK).transpose(2, 0, 1, 3)
        out = lax.map(lambda a: block(a[0], a[1], a[2]), (qb, starts, cb))
    return out.transpose(1, 0, 2, 3, 4).reshape(B, S, H * Dv)


def multiscale_pool(u, pool_w, pool_scale):
    B, S, _ = u.shape
    uf = u.astype(jnp.float32)
    cs = jnp.concatenate([jnp.zeros((B, 1, POOL_WIDTH), jnp.float32), jnp.cumsum(uf, axis=1)], axis=1)
    t = jnp.arange(S, dtype=jnp.float32)
    outs = []
    for g, w in enumerate(POOL_WINDOWS):
        sl = slice(g * POOL_GROUP_DIM, (g + 1) * POOL_GROUP_DIM)
        csg = cs[:, :, sl]
        lagged = jnp.pad(csg, ((0, 0), (w - 1, 0), (0, 0)))[:, :S]
        count = jnp.minimum(t + 1.0, float(w))[None, :, None]
        outs.append((csg[:, 1:] - lagged) / count - uf[:, :, sl])
    pooled = jnp.stack(outs, axis=2).astype(u.dtype)
    mixed = jnp.einsum('bsgc,gcd->bsgd', pooled, pool_w).reshape(B, S, POOL_WIDTH)
    return mixed * pool_scale


def conformer_conv(u, conv_w, conv_b, ln_g, ln_b):
    a, gate = jnp.split(u, 2, axis=-1)
    h = a * jax.nn.sigmoid(gate)
    h = lax.conv_general_dilated(h, conv_w, window_strides=(1,), padding=[(CONV_TAPS - 1, 0)],
                                 dimension_numbers=('NWC', 'WIO', 'NWC'),
                                 feature_group_count=CONV_WIDTH) + conv_b
    return jax.nn.silu(layer_norm(h, ln_g, ln_b))


def rope(x, positions):
    half = MLA_ROPE_DIM // 2
    inv_freq = ROPE_BASE ** (-jnp.arange(half, dtype=jnp.float32) / half)
    ang = positions.astype(jnp.float32)[..., None] * inv_freq
    cos = jnp.cos(ang)[:, :, None, :]
    sin = jnp.sin(ang)[:, :, None, :]
    xf = x.astype(jnp.float32)
    x1, x2 = xf[..., :half], xf[..., half:]
    return jnp.concatenate([x1 * cos - x2 * sin, x1 * sin + x2 * cos], axis=-1).astype(x.dtype)


def mla(c_q, c_kv, k_pe, positions, q_norm_g, w_uq, kv_norm_g, w_ukv):
    B, S, _ = c_q.shape
    q = (rms_norm(c_q, q_norm_g) @ w_uq).reshape(B, S, MLA_HEADS, MLA_NOPE_DIM + MLA_ROPE_DIM)
    q = jnp.concatenate([q[..., :MLA_NOPE_DIM], rope(q[..., MLA_NOPE_DIM:], positions)], axis=-1)
    kv = (rms_norm(c_kv, kv_norm_g) @ w_ukv).reshape(B, S, MLA_HEADS, MLA_NOPE_DIM + MLA_V_DIM)
    k_nope, v = kv[..., :MLA_NOPE_DIM], kv[..., MLA_NOPE_DIM:]
    k_rot = rope(k_pe[:, :, None, :], positions)
    k = jnp.concatenate([k_nope, jnp.broadcast_to(k_rot, (B, S, MLA_HEADS, MLA_ROPE_DIM))], axis=-1)
    return causal_block_attention(q, k, v)


def mixer_fox_pool(x, w_in, b_f, pool_w, pool_scale, w_out):
    B, S, _ = x.shape
    h = x @ w_in
    q, k, v, f_logit, u = jnp.split(
        h, [FOX_WIDTH, 2 * FOX_WIDTH, 3 * FOX_WIDTH, 3 * FOX_WIDTH + FOX_HEADS], axis=-1)
    shp = (B, S, FOX_HEADS, FOX_HEAD_DIM)
    log_f = jax.nn.log_sigmoid((f_logit + b_f).astype(jnp.float32))
    c = jnp.cumsum(log_f, axis=1)
    y_a = causal_block_attention(q.reshape(shp), k.reshape(shp), v.reshape(shp), c)
    y_b = multiscale_pool(u, pool_w, pool_scale)
    return jnp.concatenate([y_a, y_b], axis=-1) @ w_out


def mixer_conv_mla(x, positions, w_in, conv_w, conv_b, conv_ln_g, conv_ln_b,
                   q_norm_g, w_uq, kv_norm_g, w_ukv, w_out):
    h = x @ w_in
    u_c, c_q, c_kv, k_pe = jnp.split(
        h, [2 * CONV_WIDTH, 2 * CONV_WIDTH + MLA_Q_RANK, 2 * CONV_WIDTH + MLA_Q_RANK + MLA_KV_RANK], axis=-1)
    y_c = conformer_conv(u_c, conv_w, conv_b, conv_ln_g, conv_ln_b)
    y_d = mla(c_q, c_kv, k_pe, positions, q_norm_g, w_uq, kv_norm_g, w_ukv)
    return jnp.concatenate([y_c, y_d], axis=-1) @ w_out


def setup_inputs(seed: int = 0) -> dict:
    key = jax.random.key(seed)
    ks = iter(jax.random.split(key, 64))

    def normal(shape, std):
        return std * jax.random.normal(next(ks), shape, jnp.float32)

    def dense(fan_in, shape, scale=1.0):
        return normal(shape, scale * fan_in ** -0.5)

    def gain(n):
        return 1.0 + normal((n,), 0.02)

    def bias(n):
        return normal((n,), 0.02)

    inp = {}
    inp['x'] = normal((BATCH, SEQ, D_MODEL), 1.0)
    inp['positions'] = jnp.broadcast_to(jnp.arange(SEQ, dtype=jnp.int32)[None, :], (BATCH, SEQ))
    for l in range(DEPTH):
        p = 'l%d_' % l
        inp[p + 'ffn1_w1'] = dense(D_MODEL, (D_MODEL, D_FF))
        inp[p + 'ffn1_w3'] = dense(D_MODEL, (D_MODEL, D_FF))
        inp[p + 'ffn1_w2'] = dense(D_FF, (D_FF, D_MODEL), BETA)
        inp[p + 'ln_ffn1_g'] = gain(D_MODEL)
        inp[p + 'ln_ffn1_b'] = bias(D_MODEL)
        if l % 2 == 0:
            inp[p + 'w_in'] = dense(D_MODEL, (D_MODEL, IN0_WIDTH))
            inp[p + 'b_f'] = 2.0 + normal((FOX_HEADS,), 0.5)
            inp[p + 'pool_w'] = dense(POOL_GROUP_DIM, (POOL_GROUPS, POOL_GROUP_DIM, POOL_GROUP_DIM))
            inp[p + 'pool_scale'] = gain(POOL_WIDTH)
            inp[p + 'w_out'] = dense(MIX0_WIDTH, (MIX0_WIDTH, D_MODEL), BETA)
        else:
            inp[p + 'w_in'] = dense(D_MODEL, (D_MODEL, IN1_WIDTH))
            inp[p + 'conv_w'] = dense(CONV_TAPS, (CONV_TAPS, 1, CONV_WIDTH))
            inp[p + 'conv_b'] = bias(CONV_WIDTH)
            inp[p + 'conv_ln_g'] = gain(CONV_WIDTH)
            inp[p + 'conv_ln_b'] = bias(CONV_WIDTH)
            inp[p + 'q_norm_g'] = gain(MLA_Q_RANK)
            inp[p + 'w_uq'] = dense(MLA_Q_RANK, (MLA_Q_RANK, MLA_HEADS * (MLA_NOPE_DIM + MLA_ROPE_DIM)))
            inp[p + 'kv_norm_g'] = gain(MLA_KV_RANK)
            inp[p + 'w_ukv'] = dense(MLA_KV_RANK, (MLA_KV_RANK, MLA_HEADS * (MLA_NOPE_DIM + MLA_V_DIM)))
            inp[p + 'w_out'] = dense(MIX1_WIDTH, (MIX1_WIDTH, D_MODEL), BETA)
        inp[p + 'ln_mix_g'] = gain(D_MODEL)
        inp[p + 'ln_mix_b'] = bias(D_MODEL)
        inp[p + 'ffn2_w1'] = dense(D_MODEL, (D_MODEL, D_FF))
        inp[p + 'ffn2_w3'] = dense(D_MODEL, (D_MODEL, D_FF))
        inp[p + 'ffn2_w2'] = dense(D_FF, (D_FF, D_MODEL), BETA)
        inp[p + 'ln_ffn2_g'] = gain(D_MODEL)
        inp[p + 'ln_ffn2_b'] = bias(D_MODEL)
    return inp


def reference(x, positions,
              l0_ffn1_w1, l0_ffn1_w3, l0_ffn1_w2, l0_ln_ffn1_g, l0_ln_ffn1_b,
              l0_w_in, l0_b_f, l0_pool_w, l0_pool_scale, l0_w_out,
              l0_ln_mix_g, l0_ln_mix_b,
              l0_ffn2_w1, l0_ffn2_w3, l0_ffn2_w2, l0_ln_ffn2_g, l0_ln_ffn2_b,
              l1_ffn1_w1, l1_ffn1_w3, l1_ffn1_w2, l1_ln_ffn1_g, l1_ln_ffn1_b,
              l1_w_in, l1_conv_w, l1_conv_b, l1_conv_ln_g, l1_conv_ln_b,
              l1_q_norm_g, l1_w_uq, l1_kv_norm_g, l1_w_ukv, l1_w_out,
              l1_ln_mix_g, l1_ln_mix_b,
              l1_ffn2_w1, l1_ffn2_w3, l1_ffn2_w2, l1_ln_ffn2_g, l1_ln_ffn2_b):
    layers = [
        dict(ffn1=(l0_ffn1_w1, l0_ffn1_w3, l0_ffn1_w2), ln_ffn1=(l0_ln_ffn1_g, l0_ln_ffn1_b),
             mixer=(l0_w_in, l0_b_f, l0_pool_w, l0_pool_scale, l0_w_out),
             ln_mix=(l0_ln_mix_g, l0_ln_mix_b),
             ffn2=(l0_ffn2_w1, l0_ffn2_w3, l0_ffn2_w2), ln_ffn2=(l0_ln_ffn2_g, l0_ln_ffn2_b)),
        dict(ffn1=(l1_ffn1_w1, l1_ffn1_w3, l1_ffn1_w2), ln_ffn1=(l1_ln_ffn1_g, l1_ln_ffn1_b),
             mixer=(l1_w_in, l1_conv_w, l1_conv_b, l1_conv_ln_g, l1_conv_ln_b,
                    l1_q_norm_g, l1_w_uq, l1_kv_norm_g, l1_w_ukv, l1_w_out),
             ln_mix=(l1_ln_mix_g, l1_ln_mix_b),
             ffn2=(l1_ffn2_w1, l1_ffn2_w3, l1_ffn2_w2), ln_ffn2=(l1_ln_ffn2_g, l1_ln_ffn2_b)),
    ]
    for i in range(DEPTH):
        p = layers[i]
        x = layer_norm(ALPHA * x + 0.5 * swiglu(x, *p['ffn1']), *p['ln_ffn1'])
        if i % 2 == 0:
            mix = mixer_fox_pool(x, *p['mixer'])
        else:
            mix = mixer_conv_mla(x, positions, *p['mixer'])
        x = layer_norm(ALPHA * x + mix, *p['ln_mix'])
        x = layer_norm(ALPHA * x + 0.5 * swiglu(x, *p['ffn2']), *p['ln_ffn2'])
    return x
```

```python
import contextlib
import numpy as np
import ml_dtypes
import concourse.bass as bass
import concourse.mybir as mybir
from concourse.bass_utils import run_bass_kernel_spmd

F32 = mybir.dt.float32
BF16 = mybir.dt.bfloat16
I32 = mybir.dt.int32
AF = mybir.ActivationFunctionType
ALU = mybir.AluOpType
NPBF = ml_dtypes.bfloat16

ALPHA = (2 * 2) ** 0.25
LN_EPS = 1e-5


class Sem:
    def __init__(self, h):
        self.h = h
        self.n = 0


class Glob:
    def __init__(self, nc):
        self.nc = nc
        self.stack = contextlib.ExitStack()
        self.sems = {}
        self.bar = self.sem("g_bar")
        self.bar_n = 0
        self.stage_id = 0

    def sem(self, name):
        if name not in self.sems:
            self.sems[name] = Sem(self.stack.enter_context(self.nc.semaphore(name)))
        return self.sems[name]


class Rec:
    ENGS = ("sync", "act", "pe", "dve", "pool")

    def __init__(self, G, stack):
        self.G = G
        self.nc = G.nc
        self.stack = stack
        G.stage_id += 1
        self.pfx = "s%d_" % G.stage_id
        self.q = {e: [] for e in self.ENGS}
        self.esem = {e: self.sem("ev_" + e) for e in ("act", "pe", "dve", "pool")}
        if G.bar_n > 0:
            for e in self.ENGS:
                self.wait(e, (G.bar.h, G.bar_n))

    def sem(self, name):
        return self.G.sem(name)

    def sb(self, name, shape, dt):
        return self.stack.enter_context(self.nc.sbuf_tensor(self.pfx + name, shape, dt))

    def ps(self, name, shape, dt):
        return self.stack.enter_context(self.nc.psum_tensor(self.pfx + name, shape, dt))

    def wait(self, eng, ev):
        if ev is None:
            return
        h, v = ev
        self.q[eng].append(lambda e, h=h, v=v: e.wait_ge(h, v))

    def op(self, eng, fn, ev=False):
        if ev:
            s = self.esem[eng]
            s.n += 1
            self.q[eng].append(lambda e, fn=fn, h=s.h: fn(e).then_inc(h, 1))
            return (s.h, s.n)
        self.q[eng].append(fn)
        return None

    def dma(self, eng, out, in_, sem, slow=False):
        sem.n += 16
        if slow:
            self.q[eng].append(lambda e, o=out, i=in_, h=sem.h: e.dma_start(out=o, in_=i, allow_slow_non_contiguous=True).then_inc(h, 16))
        else:
            self.q[eng].append(lambda e, o=out, i=in_, h=sem.h: e.dma_start(out=o, in_=i).then_inc(h, 16))
        return (sem.h, sem.n)

    def mm(self, out, lhsT, rhs, start, stop, ev=False):
        return self.op("pe", lambda e: e.matmul(out, lhsT=lhsT, rhs=rhs, start=start, stop=stop), ev)

    def transpose(self, out, in_, ident, ev=False):
        return self.op("pe", lambda e: e.transpose(out, in_, ident), ev)

    def act(self, out, in_, func, bias=None, scale=1.0, ev=False, accum_out=None):
        kw = {}
        if bias is not None:
            kw["bias"] = bias
        if accum_out is not None:
            kw["accum_out"] = accum_out
        return self.op("act", lambda e: e.activation(out, in_, func, scale=scale, **kw), ev)

    def copy(self, eng, out, in_, ev=False):
        if eng == "act":
            return self.op("act", lambda e: e.copy(out, in_), ev)
        return self.op(eng, lambda e: e.tensor_copy(out, in_), ev)

    def tt(self, eng, out, in0, in1, op, ev=False):
        return self.op(eng, lambda e: e.tensor_tensor(out, in0, in1, op), ev)

    def ts(self, eng, out, in0, s1, s2, op0, op1=None, ev=False):
        if op1 is None:
            return self.op(eng, lambda e: e.tensor_scalar(out, in0, s1, None, op0), ev)
        return self.op(eng, lambda e: e.tensor_scalar(out, in0, s1, s2, op0, op1), ev)

    def stt(self, eng, out, in0, scalar, in1, op0, op1, ev=False):
        return self.op(eng, lambda e: e.scalar_tensor_tensor(out, in0, scalar, in1, op0, op1), ev)

    def flush(self, final_waits=()):
        for eng, ev in final_waits:
            self.wait(eng, ev)
        G = self.G
        for eng in self.ENGS:
            if eng in self.esem and self.esem[eng].n > 0:
                self.wait(eng, (self.esem[eng].h, self.esem[eng].n))
            self.q[eng].append(lambda e, h=G.bar.h: e.sem_inc(h, 1))
        G.bar_n += len(self.ENGS)
        q = self.q
        with self.nc.Block() as block:
            @block.sync
            def _(e):
                for f in q["sync"]:
                    f(e)

            @block.scalar
            def _(e):
                for f in q["act"]:
                    f(e)

            @block.tensor
            def _(e):
                for f in q["pe"]:
                    f(e)

            @block.vector
            def _(e):
                for f in q["dve"]:
                    f(e)

            @block.gpsimd
            def _(e):
                for f in q["pool"]:
                    f(e)
        self.q = {e: [] for e in self.ENGS}


class WStream:
    def __init__(self, R, name, free, nst=3, nbf=3):
        self.R = R
        self.free = free
        self.st = [R.sb(f"{name}_st{i}", [128, free], F32) for i in range(nst)]
        self.bf = [R.sb(f"{name}_bf{i}", [128, free], BF16) for i in range(nbf)]
        self.st_full = [R.sem(f"{name}_sf{i}") for i in range(nst)]
        self.st_free = [None] * nst
        self.bf_free = [None] * nbf
        self.i = 0
        self.cast_eng = ("act", "dve")

    def load(self, src_ap, view=None, n=None, scale_ap=None):
        R = self.R
        i = self.i
        self.i += 1
        s = i % len(self.st)
        b = i % len(self.bf)
        n = self.free if n is None else n
        dst = self.st[s][:, :n] if view is None else view(self.st[s])
        R.wait("sync", self.st_free[s])
        full = R.dma("sync", dst, src_ap, self.st_full[s])
        ce = self.cast_eng[i % 2]
        R.wait(ce, full)
        R.wait(ce, self.bf_free[b])
        ev = R.copy(ce, self.bf[b][:, :n], self.st[s][:, :n], ev=True)
        self.st_free[s] = ev
        return self.bf[b], ev, b

    def release(self, b, ev):
        self.bf_free[b] = ev


def stage_prep(G, x_dram, xT_dram, ident_dram, T, D):
    KC = D // 128
    with contextlib.ExitStack() as stack:
        R = Rec(G, stack)
        ident = R.sb("pp_ident", [128, 128], BF16)
        xin = [R.sb(f"pp_xin{i}", [128, D], F32) for i in range(2)]
        xb = [R.sb(f"pp_xb{i}", [128, D], BF16) for i in range(2)]
        xo = [R.sb(f"pp_xo{i}", [128, KC, 128], BF16) for i in range(2)]
        pst = [R.ps(f"pp_ps{i}", [128, 4, 128], BF16) for i in range(4)]
        s_id = R.sem("pp_sid")
        s_in = [R.sem(f"pp_sin{i}") for i in range(2)]
        s_out = [R.sem(f"pp_sout{i}") for i in range(2)]
        ev_id = R.dma("sync", ident[:], ident_dram, s_id)
        xin_free = [None, None]
        xb_free = [None, None]
        xo_free = [None, None]
        pst_free = [None] * 4
        out_evs = []
        npst = 0
        for t in range(T // 128):
            b = t % 2
            R.wait("sync", xin_free[b])
            ld = R.dma("sync", xin[b][:], x_dram[t * 128:(t + 1) * 128, :], s_in[b])
            R.wait("act", ld)
            R.wait("act", xb_free[b])
            cev = R.copy("act", xb[b][:], xin[b][:], ev=True)
            xin_free[b] = cev
            if t == 0:
                R.wait("pe", ev_id)
            R.wait("pe", cev)
            R.wait("dve", xo_free[b])
            last_tr = None
            for g in range(KC // 4):
                p = npst % 4
                npst += 1
                R.wait("pe", pst_free[p])
                for j in range(4):
                    kc = g * 4 + j
                    last_tr = R.transpose(pst[p][:, j, :], xb[b][:, kc * 128:(kc + 1) * 128], ident[:], ev=(j == 3))
                R.wait("dve", last_tr)
                pst_free[p] = R.copy("dve", xo[b][:, g * 4:(g + 1) * 4, :], pst[p][:], ev=True)
            xb_free[b] = last_tr
            R.wait("pool", pst_free[(npst - 1) % 4])
            oev = R.dma("pool", xT_dram[:, t * 128:(t + 1) * 128].rearrange("(kc p) t -> p kc t", p=128),
                        xo[b][:], s_out[b])
            R.wait("dve", None)
            xo_free[b] = oev
            out_evs.append(oev)
        R.flush(final_waits=[("pool", e) for e in out_evs[-2:]] + [("dve", e) for e in out_evs[-2:]])


def stage_ln(G, z_dram, g_dram, b_dram, x_dram, T, D, dbg=None, rms=False, eps=LN_EPS, silu=False):
    with contextlib.ExitStack() as stack:
        R = Rec(G, stack)
        gb = R.sb("ln_g", [128, D], F32)
        bb = R.sb("ln_b", [128, D], F32)
        zin = [R.sb(f"ln_z{i}", [128, D], F32) for i in range(2)]
        zo = [R.sb(f"ln_o{i}", [128, D], F32) for i in range(2)]
        junk = R.sb("ln_junk", [128, D], F32)
        sm = [R.sb(f"ln_sm{i}", [128, 8], F32) for i in range(2)]
        s_c = R.sem("ln_sc")
        s_in = [R.sem(f"ln_sin{i}") for i in range(2)]
        s_out = [R.sem(f"ln_sout{i}") for i in range(2)]
        ev_c = R.dma("sync", gb[:], g_dram.partition_broadcast(128), s_c)
        if not rms:
            ev_c = R.dma("sync", bb[:], b_dram.partition_broadcast(128), s_c)
        last = {}

        def seq(eng, fn):
            R.wait(eng, last.get(eng))
            last[eng] = R.op(eng, fn, ev=True)
            return last[eng]

        zin_free = [None, None]
        zo_free = [None, None]
        sm_free = [None, None]
        out_evs = []
        inv = 1.0 / D
        for t in range(T // 128):
            b = t % 2
            m = sm[b]
            R.wait("sync", zin_free[b])
            ld = R.dma("sync", zin[b][:], z_dram[t * 128:(t + 1) * 128, :], s_in[b])
            if t == 0:
                R.wait("pool", ev_c)
            R.wait("dve", sm_free[b])
            z0 = seq("dve", lambda e, o=m[:]: e.memset(o, 0.0))
            R.wait("act", ld)
            R.wait("act", z0)
            if not rms:
                seq("act", lambda e, o=junk[:], i=zin[b][:], a=m[:, 0:1]: e.activation(o, i, AF.Copy, accum_out=a))
            a2 = seq("act", lambda e, o=junk[:], i=zin[b][:], a=m[:, 1:2]: e.activation(o, i, AF.Square, accum_out=a))
            R.wait("dve", a2)
            seq("dve", lambda e, o=m[:, 2:3], i=m[:, 0:1]: e.tensor_scalar(o, i, inv, None, ALU.mult))
            seq("dve", lambda e, o=m[:, 3:4], i=m[:, 2:3]: e.tensor_tensor(o, i, i, ALU.mult))
            seq("dve", lambda e, o=m[:, 4:5], i=m[:, 1:2], j=m[:, 3:4]: e.scalar_tensor_tensor(o, i, inv, j, ALU.mult, ALU.subtract))
            v2 = seq("dve", lambda e, o=m[:, 5:6], i=m[:, 4:5]: e.tensor_scalar(o, i, float(eps), None, ALU.add))
            R.wait("act", v2)
            q2 = seq("act", lambda e, o=m[:, 6:7], i=m[:, 5:6]: e.sqrt(o, i))
            R.wait("dve", q2)
            seq("dve", lambda e, o=m[:, 7:8], i=m[:, 6:7]: e.reciprocal(o, i))
            R.wait("dve", zo_free[b])
            nev = seq("dve", lambda e, o=zo[b][:], i=zin[b][:], mu=m[:, 2:3], r=m[:, 7:8]: e.tensor_scalar(o, i, mu, r, ALU.subtract, ALU.mult))
            zin_free[b] = nev
            sm_free[b] = nev
            R.wait("pool", nev)
            fev = seq("pool", lambda e, o=zo[b][:], g_=gb[:]: e.tensor_tensor(o, o, g_, ALU.mult))
            if not rms:
                fev = seq("pool", lambda e, o=zo[b][:], b_=bb[:]: e.tensor_tensor(o, o, b_, ALU.add))
            if silu:
                R.wait("act", fev)
                fev = seq("act", lambda e, o=zo[b][:]: e.activation(o, o, AF.Silu))
            R.wait("sync", fev)
            oev = R.dma("sync", x_dram[t * 128:(t + 1) * 128, :], zo[b][:], s_out[b])
            zo_free[b] = oev
            out_evs.append(oev)
        R.flush(final_waits=[("sync", e) for e in out_evs[-2:]])


def stage_ffn(G, xT_dram, x_dram, w1, w3, w2, z_dram, T, D, F, res_scale=ALPHA, out_scale=0.5):
    KC = D // 128
    FC = F // 128
    KH = min(16, KC)
    NH = KC // KH
    DG = D // 512
    TT = 512
    with contextlib.ExitStack() as stack:
        R = Rec(G, stack)
        xT = R.sb("ff_xT", [128, KC, TT], BF16)
        gT = R.sb("ff_gT", [128, FC, TT], BF16)
        WA = WStream(R, "ff_wa", KH * 128, nst=2, nbf=3)
        WB = WStream(R, "ff_wb", 512, nst=4, nbf=4)
        tmp = [R.sb(f"ff_tmp{i}", [128, TT], F32) for i in range(2)]
        xr = [R.sb(f"ff_xr{i}", [128, 512], F32) for i in range(4)]
        zc = [R.sb(f"ff_zc{i}", [128, 512], F32) for i in range(4)]
        psum = [R.ps(f"ff_ps{i}", [128, 512], F32) for i in range(8)]
        s_x = R.sem("ff_sx")
        s_xr = [R.sem(f"ff_sxr{i}") for i in range(4)]
        s_zo = [R.sem(f"ff_szo{i}") for i in range(4)]
        ps_free = [None] * 8
        tmp_free = [None, None]
        xr_free = [None] * 4
        zc_free = [None] * 4
        xT_free = None
        out_evs = []
        nxr = 0
        for tile in range(T // TT):
            t0 = tile * TT
            R.wait("pool", xT_free)
            ev_x = R.dma("pool", xT[:], xT_dram[:, t0:t0 + TT].rearrange("(kc p) t -> p kc t", p=128), s_x)
            R.wait("pe", ev_x)
            last_mm = None
            for f in range(FC):
                banks = []
                for wi, w in enumerate((w1, w3)):
                    pb = (f % 2) * 2 + wi
                    R.wait("pe", ps_free[pb])
                    for h in range(NH):
                        src = w[h * KH * 128:(h + 1) * KH * 128, f * 128:(f + 1) * 128].rearrange("(kc p) n -> p kc n", p=128)
                        wt, ev, slot = WA.load(src, view=lambda t_: t_[:].rearrange("p (kc n) -> p kc n", n=128))
                        wv = wt[:].rearrange("p (kc n) -> p kc n", n=128)
                        R.wait("pe", ev)
                        for k in range(KH):
                            kc = h * KH + k
                            last = (k == KH - 1)
                            mev = R.mm(psum[pb][:], wv[:, k, :], xT[:, kc, :], start=(kc == 0), stop=(kc == KC - 1), ev=last)
                        WA.release(slot, mev)
                        last_mm = mev
                    banks.append((pb, mev))
                (pa, eva), (pbk, evb) = banks
                tb = f % 2
                R.wait("act", eva)
                R.wait("act", tmp_free[tb])
                sev = R.act(tmp[tb][:], psum[pa][:], AF.Silu, ev=True)
                ps_free[pa] = sev
                R.wait("dve", sev)
                R.wait("dve", evb)
                gev = R.tt("dve", gT[:, f, :], tmp[tb][:], psum[pbk][:], ALU.mult, ev=True)
                ps_free[pbk] = gev
                tmp_free[tb] = gev
            xT_free = last_mm
            R.wait("pe", gev)
            for dg in range(DG):
                half = (dg % 2) * 4
                xr_ev = []
                for ts_ in range(4):
                    j = nxr % 4
                    nxr += 1
                    R.wait("pool", xr_free[j])
                    xr_ev.append((j, R.dma("pool", xr[j][:], x_dram[t0 + ts_ * 128:t0 + (ts_ + 1) * 128, dg * 512:(dg + 1) * 512], s_xr[j])))
                for ts_ in range(4):
                    R.wait("pe", ps_free[half + ts_])
                for f in range(FC):
                    wt, ev, slot = WB.load(w2[f * 128:(f + 1) * 128, dg * 512:(dg + 1) * 512])
                    R.wait("pe", ev)
                    for ts_ in range(4):
                        mev = R.mm(psum[half + ts_][:], gT[:, f, ts_ * 128:(ts_ + 1) * 128], wt[:], start=(f == 0), stop=(f == FC - 1), ev=(ts_ == 3))
                    WB.release(slot, mev)
                for ts_ in range(4):
                    j, xev = xr_ev[ts_]
                    R.wait("act", xev)
                    aev = R.op("act", lambda e, o=xr[j][:], c=float(res_scale): e.mul(o, o, c), ev=True)
                    R.wait("dve", aev)
                    R.wait("dve", mev)
                    R.wait("dve", zc_free[j])
                    zev = R.stt("dve", zc[j][:], psum[half + ts_][:], float(out_scale), xr[j][:], ALU.mult, ALU.add, ev=True)
                    ps_free[half + ts_] = zev
                    xr_free[j] = zev
                    R.wait("pool", zev)
                    oev = R.dma("pool", z_dram[t0 + ts_ * 128:t0 + (ts_ + 1) * 128, dg * 512:(dg + 1) * 512], zc[j][:], s_zo[j])
                    zc_free[j] = oev
                    out_evs.append(oev)
        R.flush(final_waits=[("pool", e) for e in out_evs[-4:]] + [("dve", e) for e in out_evs[-4:]])


def stage_linear_tm(G, xT_dram, W, out_dram, T, K, N, res_dram=None, res_scale=1.0, out_scale=1.0):
    KC = K // 128
    TT = 512
    NG = (N + 511) // 512
    with contextlib.ExitStack() as stack:
        R = Rec(G, stack)
        xT = R.sb("lt_xT", [128, KC, TT], BF16)
        WB = WStream(R, "lt_wb", 512, nst=4, nbf=4)
        xr = [R.sb(f"lt_xr{i}", [128, 512], F32) for i in range(4)]
        zc = [R.sb(f"lt_zc{i}", [128, 512], F32) for i in range(4)]
        psum = [R.ps(f"lt_ps{i}", [128, 512], F32) for i in range(8)]
        s_x = R.sem("lt_sx")
        s_xr = [R.sem(f"lt_sxr{i}") for i in range(4)]
        s_zo = [R.sem(f"lt_szo{i}") for i in range(4)]
        ps_free = [None] * 8
        xr_free = [None] * 4
        zc_free = [None] * 4
        xT_free = None
        out_evs = []
        nxr = 0
        ngl = 0
        for tile in range(T // TT):
            t0 = tile * TT
            R.wait("pool", xT_free)
            ev_x = R.dma("pool", xT[:], xT_dram[:, t0:t0 + TT].rearrange("(kc p) t -> p kc t", p=128), s_x)
            R.wait("pe", ev_x)
            for ng in range(NG):
                n0 = ng * 512
                nw = min(512, N - n0)
                half = (ngl % 2) * 4
                ngl += 1
                slots = []
                for ts_ in range(4):
                    j = nxr % 4
                    nxr += 1
                    slots.append(j)
                xr_ev = [None] * 4
                if res_dram is not None:
                    for ts_ in range(4):
                        j = slots[ts_]
                        R.wait("pool", xr_free[j])
                        xr_ev[ts_] = R.dma("pool", xr[j][:, :nw], res_dram[t0 + ts_ * 128:t0 + (ts_ + 1) * 128, n0:n0 + nw], s_xr[j])
                for ts_ in range(4):
                    R.wait("pe", ps_free[half + ts_])
                for kc in range(KC):
                    wt, ev, slot = WB.load(W[kc * 128:(kc + 1) * 128, n0:n0 + nw], n=nw)
                    R.wait("pe", ev)
                    for ts_ in range(4):
                        mev = R.mm(psum[half + ts_][:, :nw], xT[:, kc, ts_ * 128:(ts_ + 1) * 128], wt[:, :nw], start=(kc == 0), stop=(kc == KC - 1), ev=(ts_ == 3))
                    WB.release(slot, mev)
                if tile == T // TT - 1 or True:
                    last_mm = mev
                for ts_ in range(4):
                    j = slots[ts_]
                    R.wait("dve", mev)
                    R.wait("dve", zc_free[j])
                    if res_dram is not None:
                        R.wait("act", xr_ev[ts_])
                        aev = R.op("act", lambda e, o=xr[j][:, :nw], c=float(res_scale): e.mul(o, o, c), ev=True)
                        R.wait("dve", aev)
                        zev = R.stt("dve", zc[j][:, :nw], psum[half + ts_][:, :nw], float(out_scale), xr[j][:, :nw], ALU.mult, ALU.add, ev=True)
                        xr_free[j] = zev
                    else:
                        zev = R.ts("dve", zc[j][:, :nw], psum[half + ts_][:, :nw], float(out_scale), None, ALU.mult, ev=True)
                    ps_free[half + ts_] = zev
                    R.wait("pool", zev)
                    oev = R.dma("pool", out_dram[t0 + ts_ * 128:t0 + (ts_ + 1) * 128, n0:n0 + nw], zc[j][:, :nw], s_zo[j])
                    zc_free[j] = oev
                    out_evs.append(oev)
            xT_free = last_mm
        R.flush(final_waits=[("pool", e) for e in out_evs[-4:]])


def stage_attn(G, qT_d, kT_d, v_d, cum_d, mask_d, oT_d, S, DK, nheads, scale, use_c):
    KCH = [(0, 128)] if DK == 128 else [(0, 128), (128, DK - 128)]
    NQT = S // 512
    NKB = S // 128
    with contextlib.ExitStack() as stack:
        R = Rec(G, stack)
        kT = [R.sb(f"at_kT{i}", [128, S], BF16) for i in range(len(KCH))]
        qt_ = [[R.sb(f"at_qt{b}_{i}", [128, 512], BF16) for i in range(len(KCH))] for b in range(2)]
        stq = [R.sb(f"at_stq{i}", [128, 512], F32) for i in range(2)]
        s_q = [R.sem(f"at_sq{i}") for i in range(2)]
        stq_free = [None, None]
        qt_free = [None, None]
        nsq = 0
        vv = R.sb("at_v", [128, NKB, 128], BF16)
        stg = [R.sb(f"at_stg{i}", [128, 2048], F32) for i in range(2)]
        negc = R.sb("at_negc", [128, NKB], F32)
        cbc = [R.sb(f"at_cbc{i}", [128, 512], F32) for i in range(2)]
        msk = R.sb("at_msk", [128, 4, 512], F32)
        ones = R.sb("at_ones", [128, 128], BF16)
        tmp = [R.sb(f"at_tmp{i}", [128, 512], F32) for i in range(3)]
        pT = [R.sb(f"at_pT{i}", [128, 512], BF16) for i in range(3)]
        rl = R.sb("at_rl", [128, 512], F32)
        ob = [R.sb(f"at_ob{i}", [128, 512], BF16) for i in range(2)]
        ps_s = [R.ps(f"at_pss{i}", [128, 512], F32) for i in range(3)]
        ps_o = [R.ps(f"at_pso{i}", [128, 512], F32) for i in range(2)]
        ps_l = [R.ps(f"at_psl{i}", [128, 512], F32) for i in range(2)]
        s_ld = [R.sem(f"at_sld{i}") for i in range(2)]
        s_c = R.sem("at_sc")
        s_cb = [R.sem(f"at_scb{i}") for i in range(2)]
        s_o = [R.sem(f"at_so{i}") for i in range(2)]
        last = {}

        def seq(eng, fn):
            R.wait(eng, last.get(eng))
            last[eng] = R.op(eng, fn, ev=True)
            return last[eng]

        ev_m = R.dma("sync", msk[:], mask_d.rearrange("d p t -> p d t"), s_c)
        o1 = seq("pool", lambda e: e.memset(ones[:], 1.0))
        stg_free = [None, None]
        nst = 0
        head_done = None
        out_evs = []
        cbc_free = [None, None]
        tmp_free = [None] * 3
        pT_free = [None] * 3
        pss_free = [None] * 3
        pso_free = [None, None]
        ob_free = [None, None]
        nblk = 0
        nqt = 0
        for h in range(nheads):
            R.wait("sync", head_done)
            R.wait("act", head_done)
            cast_evs = []
            for (dst_list, src) in ((kT, kT_d),):
                for ci, (k0, kn) in enumerate(KCH):
                    for c0 in range(0, S, 2048):
                        b = nst % 2
                        nst += 1
                        R.wait("sync", stg_free[b])
                        ld = R.dma("sync", stg[b][:kn, :], src[h, k0:k0 + kn, c0:c0 + 2048], s_ld[b])
                        R.wait("act", ld)
                        cev = seq("act", lambda e, o=dst_list[ci][:kn, c0:c0 + 2048], i=stg[b][:kn, :]: e.copy(o, i))
                        stg_free[b] = cev
                        cast_evs.append(cev)
            for j0 in range(0, NKB, 16):
                b = nst % 2
                nst += 1
                R.wait("sync", stg_free[b])
                ld = R.dma("sync", stg[b][:].rearrange("p (j d) -> p j d", d=128),
                           v_d[h, j0 * 128:(j0 + 16) * 128, :].rearrange("(j p) d -> p j d", p=128), s_ld[b])
                R.wait("act", ld)
                cev = seq("act", lambda e, o=vv[:, j0:j0 + 16, :], i=stg[b][:].rearrange("p (j d) -> p j d", d=128): e.copy(o, i))
                stg_free[b] = cev
                cast_evs.append(cev)
            if use_c:
                ev_nc = R.dma("sync", negc[:], cum_d[h, :].rearrange("(j p) -> p j", p=128), s_c, slow=True)
            R.wait("pe", cast_evs[-1])
            R.wait("pe", o1)
            for qt in range(NQT):
                q0 = qt * 512
                cb = nqt % 2
                ob_i = nqt % 2
                nqt += 1
                if use_c:
                    R.wait("sync", cbc_free[cb])
                    ev_cb = R.dma("sync", cbc[cb][:], cum_d[h, q0:q0 + 512].partition_broadcast(128), s_cb[cb])
                qb = (nqt - 1) % 2
                R.wait("act", qt_free[qb])
                for ci, (k0, kn) in enumerate(KCH):
                    sb_ = nsq % 2
                    nsq += 1
                    R.wait("sync", stq_free[sb_])
                    ldq = R.dma("sync", stq[sb_][:kn, :], qT_d[h, k0:k0 + kn, q0:q0 + 512], s_q[sb_])
                    R.wait("act", ldq)
                    qev = seq("act", lambda e, o=qt_[qb][ci][:kn, :], i=stq[sb_][:kn, :]: e.copy(o, i))
                    stq_free[sb_] = qev
                R.wait("pe", qev)
                R.wait("pe", pso_free[ob_i])
                nkb = 4 * qt + 4
                for j in range(nkb):
                    d = j - 4 * qt
                    c_lo = 128 * d if d > 0 else 0
                    w = 512 - c_lo
                    i3 = nblk % 3
                    nblk += 1
                    R.wait("pe", pss_free[i3])
                    for ci, (k0, kn) in enumerate(KCH):
                        sev = R.mm(ps_s[i3][:, c_lo:], kT[ci][:kn, j * 128:(j + 1) * 128], qt_[qb][ci][:kn, c_lo:512],
                                   start=(ci == 0), stop=(ci == len(KCH) - 1), ev=(ci == len(KCH) - 1))
                    R.wait("dve", sev)
                    R.wait("dve", tmp_free[i3])
                    if use_c:
                        R.wait("dve", ev_cb)
                        R.wait("dve", ev_nc)
                        tev = seq("dve", lambda e, o=tmp[i3][:, c_lo:], i=ps_s[i3][:, c_lo:], c=cbc[cb][:, c_lo:]: e.scalar_tensor_tensor(o, i, float(scale), c, ALU.mult, ALU.subtract))
                    else:
                        tev = seq("dve", lambda e, o=tmp[i3][:, c_lo:], i=ps_s[i3][:, c_lo:]: e.tensor_scalar(o, i, float(scale), None, ALU.mult))
                    pss_free[i3] = tev
                    if d >= 0:
                        R.wait("dve", ev_m)
                        tev = seq("dve", lambda e, o=tmp[i3][:, c_lo:], m_=msk[:, d, c_lo:]: e.tensor_tensor(o, o, m_, ALU.add))
                    R.wait("act", tev)
                    R.wait("act", pT_free[i3])
                    if use_c:
                        eev = seq("act", lambda e, o=pT[i3][:, c_lo:], i=tmp[i3][:, c_lo:], b_=negc[:, j:j + 1]: e.activation(o, i, AF.Exp, bias=b_))
                    else:
                        eev = seq("act", lambda e, o=pT[i3][:, c_lo:], i=tmp[i3][:, c_lo:]: e.activation(o, i, AF.Exp))
                    tmp_free[i3] = eev
                    R.wait("pe", eev)
                    R.mm(ps_o[ob_i][:, c_lo:], vv[:, j, :], pT[i3][:, c_lo:], start=(j == 0), stop=(j == nkb - 1))
                    pev = R.mm(ps_l[ob_i][:, c_lo:], ones[:], pT[i3][:, c_lo:], start=(j == 0), stop=(j == nkb - 1), ev=True)
                    pT_free[i3] = pev
                if use_c:
                    cbc_free[cb] = tev
                qt_free[qb] = sev
                R.wait("dve", pev)
                seq("dve", lambda e, i=ps_l[ob_i][:]: e.reciprocal(rl[:], i))
                R.wait("dve", ob_free[ob_i])
                nev = seq("dve", lambda e, o=ob[ob_i][:], i=ps_o[ob_i][:]: e.tensor_tensor(o, i, rl[:], ALU.mult))
                pso_free[ob_i] = nev
                R.wait("pool", nev)
                oev = R.dma("pool", oT_d[h, :, q0:q0 + 512], ob[ob_i][:], s_o[ob_i])
                ob_free[ob_i] = oev
                out_evs.append(oev)
            head_done = pev
        R.flush(final_waits=[("pool", e) for e in out_evs[-2:]])


def stage_negc(G, f_d, bf_d, scr_d, cum_d, nheads, S):
    NJ = S // 128
    with contextlib.ExitStack() as stack:
        R = Rec(G, stack)
        a = [R.sb(f"nc_a{i}", [128, NJ], F32) for i in range(2)]
        bb = R.sb("nc_b", [128, 1], F32)
        row = [R.sb(f"nc_row{i}", [1, 128], F32) for i in range(2)]
        off = R.sb("nc_off", [128, 1], F32)
        s1 = R.sem("nc_s1")
        last = {}

        def seq(eng, fn):
            R.wait(eng, last.get(eng))
            last[eng] = R.op(eng, fn, ev=True)
            return last[eng]

        def dma(eng, out, in_):
            R.wait(eng, last.get("dma"))
            for k in ("act", "dve"):
                R.wait(eng, last.get(k))
            last["dma"] = R.dma(eng, out, in_, s1)
            return last["dma"]

        def scan(bufs, n, width):
            cur = 0
            k = 1
            while k < width:
                src, dst = bufs[cur], bufs[1 - cur]
                seq("dve", lambda e, o=dst[:n, k:width], x=src[:n, k:width], y=src[:n, 0:width - k]: e.tensor_tensor(o, x, y, ALU.add))
                seq("dve", lambda e, o=dst[:n, 0:k], x=src[:n, 0:k]: e.tensor_copy(o, x))
                cur = 1 - cur
                k *= 2
            return cur

        ev = None
        for h in range(nheads):
            d1 = dma("sync", a[0][:], f_d[h, :].rearrange("(p j) -> p j", j=NJ))
            d2 = dma("sync", bb[:], bf_d[h:h + 1].partition_broadcast(128))
            R.wait("dve", d2)
            R.wait("dve", d1)
            seq("dve", lambda e: e.tensor_scalar(a[1][:], a[0][:], bb[:, 0:1], -1.0, ALU.add, ALU.mult))
            R.wait("act", last["dve"])
            seq("act", lambda e: e.activation(a[0][:], a[1][:], AF.Exp))
            R.wait("dve", last["act"])
            seq("dve", lambda e: e.tensor_scalar(a[1][:], a[0][:], 1.0, None, ALU.add))
            R.wait("act", last["dve"])
            seq("act", lambda e: e.activation(a[0][:], a[1][:], AF.Ln))
            R.wait("dve", last["act"])
            cur = scan(a, 128, NJ)
            res = a[cur]
            dma("sync", scr_d[h, 0, :].rearrange("(p o) -> p o", o=1), res[:, NJ - 1:NJ])
            d3 = dma("sync", row[0][:], scr_d[h, 0:1, :])
            R.wait("dve", d3)
            rc = scan(row, 1, 128)
            seq("dve", lambda e, o=row[1 - rc][0:1, 1:128], x=row[rc][0:1, 0:127]: e.tensor_copy(o, x))
            seq("dve", lambda e, o=row[1 - rc][0:1, 0:1]: e.memset(o, 0.0))
            dma("sync", scr_d[h, 1:2, :], row[1 - rc][:])
            d4 = dma("sync", off[:], scr_d[h, 1, :].rearrange("(p o) -> p o", o=1))
            R.wait("dve", d4)
            seq("dve", lambda e, o=a[1 - cur][:], x=res[:]: e.tensor_scalar(o, x, off[:, 0:1], None, ALU.add))
            ev = dma("sync", cum_d[h, :].rearrange("(p j) -> p j", j=NJ), a[1 - cur][:])
        R.flush(final_waits=[("sync", ev)])


POOL_WINDOWS = (2, 4, 8, 16)


def stage_pool(G, uT_d, invc_d, pw_d, ps_d, mixT_d, row0, T):
    H = 15
    with contextlib.ExitStack() as stack:
        R = Rec(G, stack)
        u = R.sb("pl_u", [128, H + T], F32)
        b0 = R.sb("pl_b0", [128, H + T], F32)
        b1 = R.sb("pl_b1", [128, H + T], F32)
        ic = R.sb("pl_ic", [128, T], F32)
        pooled = [R.sb(f"pl_p{i}", [128, T], BF16) for i in range(2)]
        wst = R.sb("pl_wst", [128, 2, 256], F32)
        wbf = R.sb("pl_wbf", [128, 2, 256], BF16)
        sc = R.sb("pl_sc", [128, 8], F32)
        yb = [R.sb(f"pl_y{i}", [128, 512], BF16) for i in range(2)]
        ps = [R.ps(f"pl_ps{i}", [128, 512], F32) for i in range(2)]
        s1 = R.sem("pl_s1")
        s_o = [R.sem(f"pl_so{i}") for i in range(2)]
        last = {}

        def seq(eng, fn):
            R.wait(eng, last.get(eng))
            last[eng] = R.op(eng, fn, ev=True)
            return last[eng]

        def dma_in(out, in_):
            for k in ("dve", "act", "pe"):
                R.wait("sync", last.get(k))
            return R.dma("sync", out, in_, s1)

        dsc = R.dma("sync", sc[:], ps_d.rearrange("(c p) -> p c", p=128), s1, slow=True)
        yb_free = [None, None]
        out_evs = []
        ny = 0
        for g, w in enumerate(POOL_WINDOWS):
            dic = dma_in(ic[:], invc_d[g, :].partition_broadcast(128))
            dw = dma_in(wst[:], pw_d[g].rearrange("(c p) d -> p c d", p=128))
            R.wait("act", dw)
            seq("act", lambda e: e.copy(wbf[:], wst[:]))
            for cc in range(2):
                ch0 = g * 256 + cc * 128
                du = dma_in(u[:], uT_d[ch0:ch0 + 128, :])
                R.wait("dve", du)
                R.wait("dve", dic)
                src = u
                k = 1
                bufs = [b0, b1]
                bi = 0
                while k < w:
                    dst = bufs[bi]
                    seq("dve", lambda e, o=dst[:, k:], x=src[:, k:], y=src[:, 0:H + T - k]: e.tensor_tensor(o, x, y, ALU.add))
                    src = dst
                    bi = 1 - bi
                    k *= 2
                dst = bufs[bi]
                seq("dve", lambda e, o=dst[:, H:], x=src[:, H:], y=ic[:]: e.tensor_tensor(o, x, y, ALU.mult))
                R.wait("dve", last.get("pe"))
                seq("dve", lambda e, o=pooled[cc][:], x=dst[:, H:], y=u[:, H:]: e.tensor_tensor(o, x, y, ALU.subtract))
            R.wait("pe", last["dve"])
            R.wait("pe", last["act"])
            for dc in range(2):
                for t0 in range(0, T, 512):
                    pb = ny % 2
                    ny += 1
                    R.wait("pe", yb_free[pb] if False else last.get("dve2_%d" % pb))
                    R.mm(ps[pb][:], wbf[:, 0, dc * 128:(dc + 1) * 128], pooled[0][:, t0:t0 + 512], start=True, stop=False)
                    last["pe"] = R.mm(ps[pb][:], wbf[:, 1, dc * 128:(dc + 1) * 128], pooled[1][:, t0:t0 + 512], start=False, stop=True, ev=True)
                    R.wait("dve", last["pe"])
                    R.wait("dve", dsc)
                    R.wait("dve", yb_free[pb])
                    col = g * 2 + dc
                    yev = seq("dve", lambda e, o=yb[pb][:], i=ps[pb][:], s_=sc[:, col:col + 1]: e.tensor_scalar(o, i, s_, None, ALU.mult))
                    last["dve2_%d" % pb] = yev
                    R.wait("pool", yev)
                    r0 = row0 + g * 256 + dc * 128
                    oev = R.dma("pool", mixT_d[r0:r0 + 128, t0:t0 + 512], yb[pb][:], s_o[pb])
                    yb_free[pb] = oev
                    out_evs.append(oev)
        R.flush(final_waits=[("pool", e) for e in out_evs[-2:]])


def stage_copy(G, dst, src):
    with contextlib.ExitStack() as stack:
        R = Rec(G, stack)
        sc = R.sem("cp_s")
        ev = R.dma("sync", dst, src, sc)
        R.flush(final_waits=[("sync", ev)])


FOX_H, HD = 24, 128
FOXW = FOX_H * HD
IN0 = 3 * FOXW + FOX_H + 1024


def _inp(nc, name, shape, dt=F32):
    return nc.dram_tensor(name, list(shape), dt, kind="ExternalInput").ap()


def _out(nc, name, shape, dt=F32):
    return nc.dram_tensor(name, list(shape), dt, kind="ExternalOutput").ap()


def _scr(nc, name, shape, dt=F32):
    return nc.dram_tensor(name, list(shape), dt).ap()


def _ffn_sub(G, nc, cur, dst, p, q, W, xT, z, ident, T, D, F):
    stage_prep(G, cur, xT, ident, T, D)
    stage_ffn(G, xT, cur, W[p + "w1"], W[p + "w3"], W[p + "w2"], z, T, D, F)
    stage_ln(G, z, W[q + "g"], W[q + "b"], dst, T, D)


def _decl_ffn(nc, W, p, q, D, F):
    W[p + "w1"] = _inp(nc, p + "w1", [D, F])
    W[p + "w3"] = _inp(nc, p + "w3", [D, F])
    W[p + "w2"] = _inp(nc, p + "w2", [F, D])
    W[q + "g"] = _inp(nc, q + "g", [D])
    W[q + "b"] = _inp(nc, q + "b", [D])


def prog1(cfg):
    D, F, T = cfg["D"], cfg["F"], cfg["T"]
    nc = bass.Bass("TRN2", target_bir_lowering=False)
    x = _inp(nc, "x", [T, D])
    ident = _inp(nc, "ident", [128, 128], BF16)
    W = {}
    _decl_ffn(nc, W, "l0_ffn1_", "l0_ln_ffn1_", D, F)
    w_in = _inp(nc, "l0_w_in", [D, IN0])
    x1 = _out(nc, "x1", [T, D])
    h0 = _out(nc, "h0", [T, IN0])
    xT = _scr(nc, "scr_xT", [D, T], BF16)
    z = _scr(nc, "scr_z", [T, D])
    G = Glob(nc)
    _ffn_sub(G, nc, x, x1, "l0_ffn1_", "l0_ln_ffn1_", W, xT, z, ident, T, D, F)
    stage_prep(G, x1, xT, ident, T, D)
    stage_linear_tm(G, xT, w_in, h0, T, D, IN0)
    return nc


def prog2(cfg):
    S = cfg["S"]
    nc = bass.Bass("TRN2", target_bir_lowering=False)
    qT = _inp(nc, "qT", [3, HD, S])
    kT = _inp(nc, "kT", [3, HD, S])
    v = _inp(nc, "v", [3, S, HD])
    f = _inp(nc, "f", [3, S])
    bf_ = _inp(nc, "b_f", [3])
    mask = _inp(nc, "mask", [4, 128, 512])
    oT = _out(nc, "oT", [3, HD, S], BF16)
    cum = _scr(nc, "scr_cum", [3, S])
    scr = _scr(nc, "scr_nc", [3, 2, 128])
    G = Glob(nc)
    stage_negc(G, f, bf_, scr, cum, 3, S)
    stage_attn(G, qT, kT, v, cum, mask, oT, S, HD, 3, HD ** -0.5, True)
    return nc


def prog3_l0(cfg):
    D, F, T = cfg["D"], cfg["F"], cfg["T"]
    nc = bass.Bass("TRN2", target_bir_lowering=False)
    x1 = _inp(nc, "x1", [T, D])
    yaT = _inp(nc, "yaT", [FOXW, T], BF16)
    uT = _inp(nc, "uT", [1024, 15 + T])
    invc = _inp(nc, "invc", [4, T])
    pw = _inp(nc, "l0_pool_w", [4, 256, 256])
    psc = _inp(nc, "l0_pool_scale", [1024])
    w_out = _inp(nc, "l0_w_out", [FOXW + 1024, D])
    ident = _inp(nc, "ident", [128, 128], BF16)
    W = {}
    W["l0_ln_mix_g"] = _inp(nc, "l0_ln_mix_g", [D])
    W["l0_ln_mix_b"] = _inp(nc, "l0_ln_mix_b", [D])
    _decl_ffn(nc, W, "l0_ffn2_", "l0_ln_ffn2_", D, F)
    x3 = _out(nc, "x3", [T, D])
    mixT = _scr(nc, "scr_mixT", [FOXW + 1024, T], BF16)
    xT = _scr(nc, "scr_xT", [D, T], BF16)
    z = _scr(nc, "scr_z", [T, D])
    x2 = _scr(nc, "scr_x2", [T, D])
    G = Glob(nc)
    stage_copy(G, mixT[0:FOXW, :], yaT)
    stage_pool(G, uT, invc, pw, psc, mixT, FOXW, T)
    stage_linear_tm(G, mixT, w_out, z, T, FOXW + 1024, D, res_dram=x1, res_scale=ALPHA, out_scale=1.0)
    stage_ln(G, z, W["l0_ln_mix_g"], W["l0_ln_mix_b"], x2, T, D)
    _ffn_sub(G, nc, x2, x3, "l0_ffn2_", "l0_ln_ffn2_", W, xT, z, ident, T, D, F)
    return nc


def attn_mask():
    m = np.zeros((4, 128, 512), np.float32)
    for d in range(4):
        s = 128 * d + np.arange(128)[:, None]
        t = np.arange(512)[None, :]
        m[d] = np.where(t >= s, 0.0, -30000.0)
    return m


def run_l0(inputs, cfg, ncores=8):
    D, F, S, T = cfg["D"], cfg["F"], cfg["S"], cfg["T"]
    ident = np.eye(128, dtype=np.float32).astype(NPBF)
    x = np.ascontiguousarray(inputs["x"].reshape(S, D))
    g = lambda n: np.ascontiguousarray(np.asarray(inputs[n], np.float32))
    cores = list(range(ncores))
    maps = []
    for c in cores:
        m = {n: g(n) for n in ("l0_ffn1_w1", "l0_ffn1_w3", "l0_ffn1_w2", "l0_ln_ffn1_g", "l0_ln_ffn1_b", "l0_w_in")}
        m["x"] = x[c * T:(c + 1) * T]
        m["ident"] = ident
        maps.append(m)
    r1 = run_bass_kernel_spmd(prog1(cfg), maps, core_ids=cores).results
    x1 = np.concatenate([r["x1"] for r in r1], 0)
    h0 = np.concatenate([r["h0"] for r in r1], 0)
    q, k, v = h0[:, :FOXW], h0[:, FOXW:2 * FOXW], h0[:, 2 * FOXW:3 * FOXW]
    f = h0[:, 3 * FOXW:3 * FOXW + FOX_H]
    u = h0[:, 3 * FOXW + FOX_H:]
    hm = lambda a: np.ascontiguousarray(a.reshape(S, FOX_H, HD).transpose(1, 2, 0))
    qT, kT = hm(q), hm(k)
    vh = np.ascontiguousarray(v.reshape(S, FOX_H, HD).transpose(1, 0, 2))
    fh = np.ascontiguousarray(f.T)
    mask = attn_mask()
    maps = []
    for c in cores:
        hs = slice(3 * c, 3 * c + 3)
        maps.append(dict(qT=qT[hs], kT=kT[hs], v=vh[hs], f=fh[hs], b_f=g("l0_b_f")[hs], mask=mask))
    r2 = run_bass_kernel_spmd(prog2(cfg), maps, core_ids=cores).results
    yaT = np.concatenate([r["oT"] for r in r2], 0).reshape(FOXW, S)
    uT = np.concatenate([np.zeros((1024, 15), np.float32), u.T], 1)
    maps = []
    for c in cores:
        pos = np.arange(c * T, (c + 1) * T, dtype=np.float32)
        invc = np.stack([1.0 / np.minimum(pos + 1.0, float(w)) for w in POOL_WINDOWS]).astype(np.float32)
        m = {n: g(n) for n in ("l0_pool_w", "l0_pool_scale", "l0_w_out", "l0_ln_mix_g", "l0_ln_mix_b",
                               "l0_ffn2_w1", "l0_ffn2_w3", "l0_ffn2_w2", "l0_ln_ffn2_g", "l0_ln_ffn2_b")}
        m.update(x1=x1[c * T:(c + 1) * T], yaT=np.ascontiguousarray(yaT[:, c * T:(c + 1) * T]),
                 uT=np.ascontiguousarray(uT[:, c * T:c * T + 15 + T]), invc=invc, ident=ident)
        maps.append(m)
    r3 = run_bass_kernel_spmd(prog3_l0(cfg), maps, core_ids=cores).results
    x3 = np.concatenate([r["x3"] for r in r3], 0)
    return dict(x1=x1, h0=h0, yaT=yaT, x3=x3)


TWO_PI = 2.0 * np.pi


def stage_rope(G, src_d, dst_d, pos_d, invf_d, T, nh, stride, off):
    Wd = nh * stride
    with contextlib.ExitStack() as stack:
        R = Rec(G, stack)
        xin = [R.sb(f"rp_x{i}", [128, Wd], F32) for i in range(2)]
        xo = [R.sb(f"rp_o{i}", [128, Wd], F32) for i in range(2)]
        invf = R.sb("rp_if", [128, 32], F32)
        pi_ = R.sb("rp_pi", [128, 1], I32)
        pf = R.sb("rp_pf", [128, 1], F32)
        ang = R.sb("rp_ang", [128, 32], F32)
        r1 = R.sb("rp_r1", [128, 32], F32)
        r2 = R.sb("rp_r2", [128, 32], F32)
        aa = R.sb("rp_aa", [128, 32], F32)
        kf = R.sb("rp_kf", [128, 32], F32)
        ki = R.sb("rp_ki", [128, 32], I32)
        sn = R.sb("rp_sn", [128, 32], F32)
        cs = R.sb("rp_cs", [128, 32], F32)
        t1 = R.sb("rp_t1", [128, 32], F32)
        t2 = R.sb("rp_t2", [128, 32], F32)
        mpi = R.sb("rp_mpi", [128, 1], F32)
        s1 = R.sem("rp_s1")
        s_in = [R.sem(f"rp_sin{i}") for i in range(2)]
        s_out = [R.sem(f"rp_sout{i}") for i in range(2)]
        last = {}

        def seq(eng, fn):
            R.wait(eng, last.get(eng))
            last[eng] = R.op(eng, fn, ev=True)
            return last[eng]

        e_if = R.dma("sync", invf[:], invf_d.partition_broadcast(128), s1)
        m0 = seq("dve", lambda e: e.memset(mpi[:], -float(np.pi)))
        xin_free = [None, None]
        xo_free = [None, None]
        out_evs = []
        for t in range(T // 128):
            b = t % 2
            R.wait("sync", xin_free[b])
            R.wait("sync", last.get("dve"))
            R.wait("sync", last.get("act"))
            ld = R.dma("sync", xin[b][:], src_d[t * 128:(t + 1) * 128, :], s_in[b])
            lp = R.dma("sync", pi_[:], pos_d[t * 128:(t + 1) * 128].rearrange("(p o) -> p o", o=1), s_in[b])
            R.wait("dve", lp)
            R.wait("dve", e_if)
            seq("dve", lambda e: e.tensor_copy(pf[:], pi_[:]))
            seq("dve", lambda e: e.tensor_scalar(ang[:], invf[:], pf[:, 0:1], None, ALU.mult))
            C1 = 6.28125
            C2 = float(TWO_PI - 6.28125)
            PI_LO = 3.1415925

            def sine(dst, shift):
                seq("dve", lambda e: e.tensor_scalar(aa[:], ang[:], float(shift), None, ALU.add))
                seq("dve", lambda e: e.tensor_scalar(r1[:], aa[:], float(1.0 / TWO_PI), None, ALU.mult))
                seq("dve", lambda e: e.tensor_copy(ki[:], r1[:]))
                seq("dve", lambda e: e.tensor_copy(kf[:], ki[:]))
                seq("dve", lambda e: e.scalar_tensor_tensor(r1[:], kf[:], -C1, aa[:], ALU.mult, ALU.add))
                seq("dve", lambda e: e.scalar_tensor_tensor(r2[:], kf[:], -C2, r1[:], ALU.mult, ALU.add))
                seq("dve", lambda e: e.tensor_scalar(t1[:], r2[:], float(np.pi), -TWO_PI, ALU.is_gt, ALU.mult))
                seq("dve", lambda e: e.tensor_tensor(r1[:], r2[:], t1[:], ALU.add))
                seq("dve", lambda e: e.tensor_scalar(t1[:], r1[:], -float(np.pi), TWO_PI, ALU.is_lt, ALU.mult))
                seq("dve", lambda e: e.tensor_tensor(r2[:], r1[:], t1[:], ALU.add))
                seq("dve", lambda e: e.tensor_scalar_max(r1[:], r2[:], -PI_LO))
                seq("dve", lambda e: e.tensor_scalar_min(r2[:], r1[:], PI_LO))
                R.wait("act", last["dve"])
                seq("act", lambda e: e.activation(dst[:], r2[:], AF.Sin))
                R.wait("dve", last["act"])

            sine(sn, 0.0)
            sine(cs, 0.5 * np.pi)
            R.wait("dve", ld)
            R.wait("dve", xo_free[b])
            seq("dve", lambda e, o=xo[b][:], i=xin[b][:]: e.tensor_copy(o, i))
            for h in range(nh):
                c0 = h * stride + off
                x1 = xin[b][:, c0:c0 + 32]
                x2 = xin[b][:, c0 + 32:c0 + 64]
                seq("dve", lambda e, a=x1: e.tensor_tensor(t1[:], a, cs[:], ALU.mult))
                seq("dve", lambda e, a=x2: e.tensor_tensor(t2[:], a, sn[:], ALU.mult))
                seq("dve", lambda e, o=xo[b][:, c0:c0 + 32]: e.tensor_tensor(o, t1[:], t2[:], ALU.subtract))
                seq("dve", lambda e, a=x1: e.tensor_tensor(t1[:], a, sn[:], ALU.mult))
                seq("dve", lambda e, a=x2: e.tensor_tensor(t2[:], a, cs[:], ALU.mult))
                seq("dve", lambda e, o=xo[b][:, c0 + 32:c0 + 64]: e.tensor_tensor(o, t1[:], t2[:], ALU.add))
            xin_free[b] = last["dve"]
            R.wait("pool", last["dve"])
            oev = R.dma("pool", dst_d[t * 128:(t + 1) * 128, :], xo[b][:], s_out[b])
            xo_free[b] = oev
            out_evs.append(oev)
        R.flush(final_waits=[("pool", e) for e in out_evs[-2:]])


def stage_conv(G, aT_d, gT_d, w_d, b_d, out_d, S, taps=31):
    H = taps - 1
    CH = 2048
    with contextlib.ExitStack() as stack:
        R = Rec(G, stack)
        a = R.sb("cv_a", [128, H + CH], F32)
        g = R.sb("cv_g", [128, H + CH], F32)
        acc = [R.sb(f"cv_acc{i}", [128, CH], F32) for i in range(2)]
        w = R.sb("cv_w", [128, taps], F32)
        bb = R.sb("cv_b", [128, 1], F32)
        s1 = R.sem("cv_s1")
        s_o = [R.sem(f"cv_so{i}") for i in range(2)]
        last = {}

        def seq(eng, fn):
            R.wait(eng, last.get(eng))
            last[eng] = R.op(eng, fn, ev=True)
            return last[eng]

        R.dma("sync", w[:], w_d, s1)
        e_w = R.dma("sync", bb[:], b_d, s1)
        acc_free = [None, None]
        out_evs = []
        for ci, c0 in enumerate(range(0, S, CH)):
            b = ci % 2
            R.wait("sync", last.get("dve"))
            R.wait("sync", last.get("act"))
            R.dma("sync", a[:], aT_d[:, c0:c0 + H + CH], s1)
            ld = R.dma("sync", g[:], gT_d[:, c0:c0 + H + CH], s1)
            R.wait("act", ld)
            seq("act", lambda e: e.activation(g[:], g[:], AF.Sigmoid))
            R.wait("dve", last["act"])
            R.wait("dve", e_w)
            seq("dve", lambda e: e.tensor_tensor(a[:], a[:], g[:], ALU.mult))
            R.wait("dve", acc_free[b])
            seq("dve", lambda e, o=acc[b][:]: e.tensor_scalar(o, a[:, 0:CH], w[:, 0:1], bb[:, 0:1], ALU.mult, ALU.add))
            for k in range(1, taps):
                seq("dve", lambda e, o=acc[b][:], k=k: e.scalar_tensor_tensor(o, a[:, k:k + CH], w[:, k:k + 1], o, ALU.mult, ALU.add))
            R.wait("pool", last["dve"])
            oev = R.dma("pool", out_d[:, c0:c0 + CH], acc[b][:], s_o[b])
            acc_free[b] = oev
            out_evs.append(oev)
        R.flush(final_waits=[("pool", e) for e in out_evs[-2:]])


MLA_H = 24
IN1 = 2 * 1024 + 1024 + 512 + 64
QW = MLA_H * 192
KVW = MLA_H * 256
RMS_EPS = 1e-6


def prog3(cfg):
    D, F, T = cfg["D"], cfg["F"], cfg["T"]
    nc = bass.Bass("TRN2", target_bir_lowering=False)
    x1 = _inp(nc, "x1", [T, D])
    yaT = _inp(nc, "yaT", [FOXW, T], BF16)
    uT = _inp(nc, "uT", [1024, 15 + T])
    invc = _inp(nc, "invc", [4, T])
    pw = _inp(nc, "l0_pool_w", [4, 256, 256])
    psc = _inp(nc, "l0_pool_scale", [1024])
    w_out = _inp(nc, "l0_w_out", [FOXW + 1024, D])
    ident = _inp(nc, "ident", [128, 128], BF16)
    pos = _inp(nc, "pos", [T], I32)
    invf = _inp(nc, "invf", [32])
    W = {}
    for n in ("l0_ln_mix_g", "l0_ln_mix_b"):
        W[n] = _inp(nc, n, [D])
    _decl_ffn(nc, W, "l0_ffn2_", "l0_ln_ffn2_", D, F)
    _decl_ffn(nc, W, "l1_ffn1_", "l1_ln_ffn1_", D, F)
    w_in1 = _inp(nc, "l1_w_in", [D, IN1])
    qg = _inp(nc, "l1_q_norm_g", [1024])
    kvg = _inp(nc, "l1_kv_norm_g", [512])
    w_uq = _inp(nc, "l1_w_uq", [1024, QW])
    w_ukv = _inp(nc, "l1_w_ukv", [512, KVW])
    x4 = _out(nc, "x4", [T, D])
    h1 = _out(nc, "h1", [T, IN1])
    qr = _out(nc, "qr", [T, QW])
    kvf = _out(nc, "kvf", [T, KVW])
    kper = _out(nc, "kper", [T, 64])
    mixT = _scr(nc, "scr_mixT", [FOXW + 1024, T], BF16)
    xT = _scr(nc, "scr_xT", [D, T], BF16)
    xTq = _scr(nc, "scr_xTq", [1024, T], BF16)
    xTkv = _scr(nc, "scr_xTkv", [512, T], BF16)
    z = _scr(nc, "scr_z", [T, D])
    x2 = _scr(nc, "scr_x2", [T, D])
    x3 = _scr(nc, "scr_x3", [T, D])
    cqn = _scr(nc, "scr_cqn", [T, 1024])
    ckvn = _scr(nc, "scr_ckvn", [T, 512])
    qf = _scr(nc, "scr_qf", [T, QW])
    G = Glob(nc)
    stage_copy(G, mixT[0:FOXW, :], yaT)
    stage_pool(G, uT, invc, pw, psc, mixT, FOXW, T)
    stage_linear_tm(G, mixT, w_out, z, T, FOXW + 1024, D, res_dram=x1, res_scale=ALPHA, out_scale=1.0)
    stage_ln(G, z, W["l0_ln_mix_g"], W["l0_ln_mix_b"], x2, T, D)
    _ffn_sub(G, nc, x2, x3, "l0_ffn2_", "l0_ln_ffn2_", W, xT, z, ident, T, D, F)
    _ffn_sub(G, nc, x3, x4, "l1_ffn1_", "l1_ln_ffn1_", W, xT, z, ident, T, D, F)
    stage_prep(G, x4, xT, ident, T, D)
    stage_linear_tm(G, xT, w_in1, h1, T, D, IN1)
    stage_ln(G, h1[:, 2048:3072], qg, None, cqn, T, 1024, rms=True, eps=RMS_EPS)
    stage_prep(G, cqn, xTq, ident, T, 1024)
    stage_linear_tm(G, xTq, w_uq, qf, T, 1024, QW)
    stage_rope(G, qf, qr, pos, invf, T, MLA_H, 192, 128)
    stage_ln(G, h1[:, 3072:3584], kvg, None, ckvn, T, 512, rms=True, eps=RMS_EPS)
    stage_prep(G, ckvn, xTkv, ident, T, 512)
    stage_linear_tm(G, xTkv, w_ukv, kvf, T, 512, KVW)
    stage_rope(G, h1[:, 3584:3648], kper, pos, invf, T, 1, 64, 0)
    return nc


def prog4(cfg):
    S = cfg["S"]
    nc = bass.Bass("TRN2", target_bir_lowering=False)
    qT = _inp(nc, "qT", [3, 192, S])
    kT = _inp(nc, "kT", [3, 192, S])
    v = _inp(nc, "v", [3, S, HD])
    mask = _inp(nc, "mask", [4, 128, 512])
    aT = _inp(nc, "aT", [128, 30 + S])
    gT = _inp(nc, "gT", [128, 30 + S])
    cw = _inp(nc, "cw", [128, 31])
    cb = _inp(nc, "cb", [128, 1])
    oT = _out(nc, "oT", [3, HD, S], BF16)
    cvT = _out(nc, "cvT", [128, S])
    G = Glob(nc)
    stage_conv(G, aT, gT, cw, cb, cvT, S)
    stage_attn(G, qT, kT, v, None, mask, oT, S, 192, 3, 192 ** -0.5, False)
    return nc


def prog5(cfg):
    D, F, T = cfg["D"], cfg["F"], cfg["T"]
    nc = bass.Bass("TRN2", target_bir_lowering=False)
    x4 = _inp(nc, "x4", [T, D])
    cvt = _inp(nc, "cvt", [T, 1024])
    ydT = _inp(nc, "ydT", [MLA_H * HD, T], BF16)
    clg = _inp(nc, "l1_conv_ln_g", [1024])
    clb = _inp(nc, "l1_conv_ln_b", [1024])
    w_out = _inp(nc, "l1_w_out", [1024 + MLA_H * HD, D])
    ident = _inp(nc, "ident", [128, 128], BF16)
    W = {}
    for n in ("l1_ln_mix_g", "l1_ln_mix_b"):
        W[n] = _inp(nc, n, [D])
    _decl_ffn(nc, W, "l1_ffn2_", "l1_ln_ffn2_", D, F)
    out = _out(nc, "out", [T, D])
    mixT = _scr(nc, "scr_mixT", [1024 + MLA_H * HD, T], BF16)
    xT = _scr(nc, "scr_xT", [D, T], BF16)
    z = _scr(nc, "scr_z", [T, D])
    x5 = _scr(nc, "scr_x5", [T, D])
    yc = _scr(nc, "scr_yc", [T, 1024])
    G = Glob(nc)
    stage_ln(G, cvt, clg, clb, yc, T, 1024, silu=True)
    stage_prep(G, yc, mixT[0:1024, :], ident, T, 1024)
    stage_copy(G, mixT[1024:1024 + MLA_H * HD, :], ydT)
    stage_linear_tm(G, mixT, w_out, z, T, 1024 + MLA_H * HD, D, res_dram=x4, res_scale=ALPHA, out_scale=1.0)
    stage_ln(G, z, W["l1_ln_mix_g"], W["l1_ln_mix_b"], x5, T, D)
    _ffn_sub(G, nc, x5, out, "l1_ffn2_", "l1_ln_ffn2_", W, xT, z, ident, T, D, F)
    return nc


def run_all(inputs, cfg, ncores=8, keep=False):
    D, F, S, T = cfg["D"], cfg["F"], cfg["S"], cfg["T"]
    ident = np.eye(128, dtype=np.float32).astype(NPBF)
    x = np.ascontiguousarray(np.asarray(inputs["x"], np.float32).reshape(S, D))
    g = lambda n: np.ascontiguousarray(np.asarray(inputs[n], np.float32))
    cores = list(range(ncores))
    mask = attn_mask()
    invf = (10000.0 ** (-np.arange(32, dtype=np.float32) / np.float32(32))).astype(np.float32)
    posi = np.ascontiguousarray(np.asarray(inputs["positions"]).reshape(S).astype(np.int32))
    maps = []
    for c in cores:
        m = {n: g(n) for n in ("l0_ffn1_w1", "l0_ffn1_w3", "l0_ffn1_w2", "l0_ln_ffn1_g", "l0_ln_ffn1_b", "l0_w_in")}
        m["x"] = x[c * T:(c + 1) * T]
        m["ident"] = ident
        maps.append(m)
    r1 = run_bass_kernel_spmd(prog1(cfg), maps, core_ids=cores).results
    x1 = np.concatenate([r["x1"] for r in r1], 0)
    h0 = np.concatenate([r["h0"] for r in r1], 0)
    del r1
    hm = lambda a: np.ascontiguousarray(a.reshape(S, FOX_H, HD).transpose(1, 2, 0))
    qT, kT = hm(h0[:, :FOXW]), hm(h0[:, FOXW:2 * FOXW])
    vh = np.ascontiguousarray(h0[:, 2 * FOXW:3 * FOXW].reshape(S, FOX_H, HD).transpose(1, 0, 2))
    fh = np.ascontiguousarray(h0[:, 3 * FOXW:3 * FOXW + FOX_H].T)
    uT = np.concatenate([np.zeros((1024, 15), np.float32), h0[:, 3 * FOXW + FOX_H:].T], 1)
    del h0
    maps = []
    for c in cores:
        hs = slice(3 * c, 3 * c + 3)
        maps.append(dict(qT=qT[hs], kT=kT[hs], v=vh[hs], f=fh[hs], b_f=g("l0_b_f")[hs], mask=mask))
    r2 = run_bass_kernel_spmd(prog2(cfg), maps, core_ids=cores).results
    yaT = np.concatenate([r["oT"] for r in r2], 0).reshape(FOXW, S)
    del r2, qT, kT, vh
    names3 = ("l0_pool_w", "l0_pool_scale", "l0_w_out", "l0_ln_mix_g", "l0_ln_mix_b",
              "l0_ffn2_w1", "l0_ffn2_w3", "l0_ffn2_w2", "l0_ln_ffn2_g", "l0_ln_ffn2_b",
              "l1_ffn1_w1", "l1_ffn1_w3", "l1_ffn1_w2", "l1_ln_ffn1_g", "l1_ln_ffn1_b",
              "l1_w_in", "l1_q_norm_g", "l1_kv_norm_g", "l1_w_uq", "l1_w_ukv")
    maps = []
    for c in cores:
        p = np.arange(c * T, (c + 1) * T, dtype=np.float32)
        invc = np.stack([1.0 / np.minimum(p + 1.0, float(w)) for w in POOL_WINDOWS]).astype(np.float32)
        m = {n: g(n) for n in names3}
        m.update(x1=x1[c * T:(c + 1) * T], yaT=np.ascontiguousarray(yaT[:, c * T:(c + 1) * T]),
                 uT=np.ascontiguousarray(uT[:, c * T:c * T + 15 + T]), invc=invc, ident=ident,
                 pos=posi[c * T:(c + 1) * T], invf=invf)
        maps.append(m)
    r3 = run_bass_kernel_spmd(prog3(cfg), maps, core_ids=cores).results
    cat = lambda k: np.concatenate([r[k] for r in r3], 0)
    x4, h1, qr, kvf, kper = cat("x4"), cat("h1"), cat("qr"), cat("kvf"), cat("kper")
    del r3, x1, yaT, uT
    qT = np.ascontiguousarray(qr.reshape(S, MLA_H, 192).transpose(1, 2, 0))
    kv = kvf.reshape(S, MLA_H, 256)
    kT = np.empty((MLA_H, 192, S), np.float32)
    kT[:, :128, :] = kv[:, :, :128].transpose(1, 2, 0)
    kT[:, 128:, :] = kper.T[None, :, :]
    vh = np.ascontiguousarray(kv[:, :, 128:].transpose(1, 0, 2))
    aT = np.concatenate([np.zeros((1024, 30), np.float32), h1[:, :1024].T], 1)
    gT = np.concatenate([np.zeros((1024, 30), np.float32), h1[:, 1024:2048].T], 1)
    cw = np.ascontiguousarray(g("l1_conv_w")[:, 0, :].T)
    cb = g("l1_conv_b").reshape(1024, 1)
    maps = []
    for c in cores:
        hs = slice(3 * c, 3 * c + 3)
        cs_ = slice(128 * c, 128 * (c + 1))
        maps.append(dict(qT=qT[hs], kT=kT[hs], v=vh[hs], mask=mask, aT=np.ascontiguousarray(aT[cs_]),
                         gT=np.ascontiguousarray(gT[cs_]), cw=np.ascontiguousarray(cw[cs_]), cb=np.ascontiguousarray(cb[cs_])))
    r4 = run_bass_kernel_spmd(prog4(cfg), maps, core_ids=cores).results
    ydT = np.concatenate([r["oT"] for r in r4], 0).reshape(MLA_H * HD, S)
    cvt = np.ascontiguousarray(np.concatenate([r["cvT"] for r in r4], 0).T)
    del r4, qT, kT, vh, aT, gT
    names5 = ("l1_conv_ln_g", "l1_conv_ln_b", "l1_w_out", "l1_ln_mix_g", "l1_ln_mix_b",
              "l1_ffn2_w1", "l1_ffn2_w3", "l1_ffn2_w2", "l1_ln_ffn2_g", "l1_ln_ffn2_b")
    maps = []
    for c in cores:
        m = {n: g(n) for n in names5}
        m.update(x4=x4[c * T:(c + 1) * T], cvt=cvt[c * T:(c + 1) * T],
                 ydT=np.ascontiguousarray(ydT[:, c * T:(c + 1) * T]), ident=ident)
        maps.append(m)
    r5 = run_bass_kernel_spmd(prog5(cfg), maps, core_ids=cores).results
    out = np.concatenate([r["out"] for r in r5], 0)
    if keep:
        return dict(out=out, x4=x4, h1=h1, qr=qr, kvf=kvf, kper=kper, ydT=ydT, cvt=cvt)
    return out


CFG_FULL = dict(D=4096, F=11008, S=16384, T=2048)


def kernel(**inputs):
    out = run_all(inputs, CFG_FULL, ncores=8)
    return np.ascontiguousarray(out.reshape(1, CFG_FULL["S"], CFG_FULL["D"]).astype(np.float32))
```

```python
import contextlib
import numpy as np
import ml_dtypes
import concourse.bass as bass
import concourse.mybir as mybir
from concourse.bass_utils import run_bass_kernel_spmd

F32 = mybir.dt.float32
BF16 = mybir.dt.bfloat16
I32 = mybir.dt.int32
AF = mybir.ActivationFunctionType
ALU = mybir.AluOpType
NPBF = ml_dtypes.bfloat16

ALPHA = (2 * 2) ** 0.25
LN_EPS = 1e-5


class Sem:
    def __init__(self, h):
        self.h = h
        self.n = 0


class Glob:
    def __init__(self, nc):
        self.nc = nc
        self.stack = contextlib.ExitStack()
        self.sems = {}
        self.bar = self.sem("g_bar")
        self.bar_n = 0
        self.stage_id = 0

    def sem(self, name):
        if name not in self.sems:
            self.sems[name] = Sem(self.stack.enter_context(self.nc.semaphore(name)))
        return self.sems[name]


class Rec:
    ENGS = ("sync", "act", "pe", "dve", "pool")

    def __init__(self, G, stack):
        self.G = G
        self.nc = G.nc
        self.stack = stack
        G.stage_id += 1
        self.pfx = "s%d_" % G.stage_id
        self.q = {e: [] for e in self.ENGS}
        self.esem = {e: self.sem("ev_" + e) for e in ("act", "pe", "dve", "pool")}
        if G.bar_n > 0:
            for e in self.ENGS:
                self.wait(e, (G.bar.h, G.bar_n))

    def sem(self, name):
        return self.G.sem(name)

    def sb(self, name, shape, dt):
        return self.stack.enter_context(self.nc.sbuf_tensor(self.pfx + name, shape, dt))

    def ps(self, name, shape, dt):
        return self.stack.enter_context(self.nc.psum_tensor(self.pfx + name, shape, dt))

    def wait(self, eng, ev):
        if ev is None:
            return
        h, v = ev
        self.q[eng].append(lambda e, h=h, v=v: e.wait_ge(h, v))

    def op(self, eng, fn, ev=False):
        if ev:
            s = self.esem[eng]
            s.n += 1
            self.q[eng].append(lambda e, fn=fn, h=s.h: fn(e).then_inc(h, 1))
            return (s.h, s.n)
        self.q[eng].append(fn)
        return None

    def dma(self, eng, out, in_, sem, slow=False):
        sem.n += 16
        if slow:
            self.q[eng].append(lambda e, o=out, i=in_, h=sem.h: e.dma_start(out=o, in_=i, allow_slow_non_contiguous=True).then_inc(h, 16))
        else:
            self.q[eng].append(lambda e, o=out, i=in_, h=sem.h: e.dma_start(out=o, in_=i).then_inc(h, 16))
        return (sem.h, sem.n)

    def mm(self, out, lhsT, rhs, start, stop, ev=False):
        return self.op("pe", lambda e: e.matmul(out, lhsT=lhsT, rhs=rhs, start=start, stop=stop), ev)

    def transpose(self, out, in_, ident, ev=False):
        return self.op("pe", lambda e: e.transpose(out, in_, ident), ev)

    def act(self, out, in_, func, bias=None, scale=1.0, ev=False, accum_out=None):
        kw = {}
        if bias is not None:
            kw["bias"] = bias
        if accum_out is not None:
            kw["accum_out"] = accum_out
        return self.op("act", lambda e: e.activation(out, in_, func, scale=scale, **kw), ev)

    def copy(self, eng, out, in_, ev=False):
        if eng == "act":
            return self.op("act", lambda e: e.copy(out, in_), ev)
        return self.op(eng, lambda e: e.tensor_copy(out, in_), ev)

    def tt(self, eng, out, in0, in1, op, ev=False):
        return self.op(eng, lambda e: e.tensor_tensor(out, in0, in1, op), ev)

    def ts(self, eng, out, in0, s1, s2, op0, op1=None, ev=False):
        if op1 is None:
            return self.op(eng, lambda e: e.tensor_scalar(out, in0, s1, None, op0), ev)
        return self.op(eng, lambda e: e.tensor_scalar(out, in0, s1, s2, op0, op1), ev)

    def stt(self, eng, out, in0, scalar, in1, op0, op1, ev=False):
        return self.op(eng, lambda e: e.scalar_tensor_tensor(out, in0, scalar, in1, op0, op1), ev)

    def flush(self, final_waits=()):
        for eng, ev in final_waits:
            self.wait(eng, ev)
        G = self.G
        for eng in self.ENGS:
            if eng in self.esem and self.esem[eng].n > 0:
                self.wait(eng, (self.esem[eng].h, self.esem[eng].n))
            self.q[eng].append(lambda e, h=G.bar.h: e.sem_inc(h, 1))
        G.bar_n += len(self.ENGS)
        q = self.q
        with self.nc.Block() as block:
            @block.sync
            def _(e):
                for f in q["sync"]:
                    f(e)

            @block.scalar
            def _(e):
                for f in q["act"]:
                    f(e)

            @block.tensor
            def _(e):
                for f in q["pe"]:
                    f(e)

            @block.vector
            def _(e):
                for f in q["dve"]:
                    f(e)

            @block.gpsimd
            def _(e):
                for f in q["pool"]:
                    f(e)
        self.q = {e: [] for e in self.ENGS}


class WStream:
    def __init__(self, R, name, free, nst=3, nbf=3):
        self.R = R
        self.free = free
        self.st = [R.sb(f"{name}_st{i}", [128, free], F32) for i in range(nst)]
        self.bf = [R.sb(f"{name}_bf{i}", [128, free], BF16) for i in range(nbf)]
        self.st_full = [R.sem(f"{name}_sf{i}") for i in range(nst)]
        self.st_free = [None] * nst
        self.bf_free = [None] * nbf
        self.i = 0
        self.cast_eng = ("act", "dve")
        self.dma_eng = ("sync",)

    def load(self, src_ap, view=None, n=None, scale_ap=None):
        R = self.R
        i = self.i
        self.i += 1
        s = i % len(self.st)
        b = i % len(self.bf)
        n = self.free if n is None else n
        dst = self.st[s][:, :n] if view is None else view(self.st[s])
        de = self.dma_eng[i % len(self.dma_eng)]
        R.wait(de, self.st_free[s])
        full = R.dma(de, dst, src_ap, self.st_full[s])
        ce = self.cast_eng[i % 2]
        R.wait(ce, full)
        R.wait(ce, self.bf_free[b])
        ev = R.copy(ce, self.bf[b][:, :n], self.st[s][:, :n], ev=True)
        self.st_free[s] = ev
        return self.bf[b], ev, b

    def release(self, b, ev):
        self.bf_free[b] = ev


def stage_prep(G, x_dram, xT_dram, ident_dram, T, D):
    KC = D // 128
    with contextlib.ExitStack() as stack:
        R = Rec(G, stack)
        ident = R.sb("pp_ident", [128, 128], BF16)
        xin = [R.sb(f"pp_xin{i}", [128, D], F32) for i in range(2)]
        xb = [R.sb(f"pp_xb{i}", [128, D], BF16) for i in range(2)]
        xo = [R.sb(f"pp_xo{i}", [128, KC, 128], BF16) for i in range(2)]
        pst = [R.ps(f"pp_ps{i}", [128, 4, 128], BF16) for i in range(4)]
        s_id = R.sem("pp_sid")
        s_in = [R.sem(f"pp_sin{i}") for i in range(2)]
        s_out = [R.sem(f"pp_sout{i}") for i in range(2)]
        ev_id = R.dma("sync", ident[:], ident_dram, s_id)
        xin_free = [None, None]
        xb_free = [None, None]
        xo_free = [None, None]
        pst_free = [None] * 4
        out_evs = []
        npst = 0
        for t in range(T // 128):
            b = t % 2
            R.wait("sync", xin_free[b])
            ld = R.dma("sync", xin[b][:], x_dram[t * 128:(t + 1) * 128, :], s_in[b])
            R.wait("act", ld)
            R.wait("act", xb_free[b])
            cev = R.copy("act", xb[b][:], xin[b][:], ev=True)
            xin_free[b] = cev
            if t == 0:
                R.wait("pe", ev_id)
            R.wait("pe", cev)
            R.wait("dve", xo_free[b])
            last_tr = None
            for g in range(KC // 4):
                p = npst % 4
                npst += 1
                R.wait("pe", pst_free[p])
                for j in range(4):
                    kc = g * 4 + j
                    last_tr = R.transpose(pst[p][:, j, :], xb[b][:, kc * 128:(kc + 1) * 128], ident[:], ev=(j == 3))
                R.wait("dve", last_tr)
                pst_free[p] = R.copy("dve", xo[b][:, g * 4:(g + 1) * 4, :], pst[p][:], ev=True)
            xb_free[b] = last_tr
            R.wait("pool", pst_free[(npst - 1) % 4])
            oev = R.dma("pool", xT_dram[:, t * 128:(t + 1) * 128].rearrange("(kc p) t -> p kc t", p=128),
                        xo[b][:], s_out[b])
            R.wait("dve", None)
            xo_free[b] = oev
            out_evs.append(oev)
        R.flush(final_waits=[("pool", e) for e in out_evs[-2:]] + [("dve", e) for e in out_evs[-2:]])


def stage_ln(G, z_dram, g_dram, b_dram, x_dram, T, D, dbg=None, rms=False, eps=LN_EPS, silu=False):
    with contextlib.ExitStack() as stack:
        R = Rec(G, stack)
        gb = R.sb("ln_g", [128, D], F32)
        bb = R.sb("ln_b", [128, D], F32)
        zin = [R.sb(f"ln_z{i}", [128, D], F32) for i in range(2)]
        zo = [R.sb(f"ln_o{i}", [128, D], F32) for i in range(2)]
        junk = R.sb("ln_junk", [128, D], F32)
        sm = [R.sb(f"ln_sm{i}", [128, 8], F32) for i in range(2)]
        s_c = R.sem("ln_sc")
        s_in = [R.sem(f"ln_sin{i}") for i in range(2)]
        s_out = [R.sem(f"ln_sout{i}") for i in range(2)]
        ev_c = R.dma("sync", gb[:], g_dram.partition_broadcast(128), s_c)
        if not rms:
            ev_c = R.dma("sync", bb[:], b_dram.partition_broadcast(128), s_c)
        last = {}

        def seq(eng, fn):
            R.wait(eng, last.get(eng))
            last[eng] = R.op(eng, fn, ev=True)
            return last[eng]

        zin_free = [None, None]
        zo_free = [None, None]
        sm_free = [None, None]
        out_evs = []
        inv = 1.0 / D
        for t in range(T // 128):
            b = t % 2
            m = sm[b]
            R.wait("sync", zin_free[b])
            ld = R.dma("sync", zin[b][:], z_dram[t * 128:(t + 1) * 128, :], s_in[b])
            if t == 0:
                R.wait("pool", ev_c)
            R.wait("dve", sm_free[b])
            z0 = seq("dve", lambda e, o=m[:]: e.memset(o, 0.0))
            R.wait("act", ld)
            R.wait("act", z0)
            if not rms:
                seq("act", lambda e, o=junk[:], i=zin[b][:], a=m[:, 0:1]: e.activation(o, i, AF.Copy, accum_out=a))
            a2 = seq("act", lambda e, o=junk[:], i=zin[b][:], a=m[:, 1:2]: e.activation(o, i, AF.Square, accum_out=a))
            R.wait("dve", a2)
            seq("dve", lambda e, o=m[:, 2:3], i=m[:, 0:1]: e.tensor_scalar(o, i, inv, None, ALU.mult))
            seq("dve", lambda e, o=m[:, 3:4], i=m[:, 2:3]: e.tensor_tensor(o, i, i, ALU.mult))
            seq("dve", lambda e, o=m[:, 4:5], i=m[:, 1:2], j=m[:, 3:4]: e.scalar_tensor_tensor(o, i, inv, j, ALU.mult, ALU.subtract))
            v2 = seq("dve", lambda e, o=m[:, 5:6], i=m[:, 4:5]: e.tensor_scalar(o, i, float(eps), None, ALU.add))
            R.wait("act", v2)
            q2 = seq("act", lambda e, o=m[:, 6:7], i=m[:, 5:6]: e.sqrt(o, i))
            R.wait("dve", q2)
            seq("dve", lambda e, o=m[:, 7:8], i=m[:, 6:7]: e.reciprocal(o, i))
            R.wait("dve", zo_free[b])
            nev = seq("dve", lambda e, o=zo[b][:], i=zin[b][:], mu=m[:, 2:3], r=m[:, 7:8]: e.tensor_scalar(o, i, mu, r, ALU.subtract, ALU.mult))
            zin_free[b] = nev
            sm_free[b] = nev
            R.wait("pool", nev)
            fev = seq("pool", lambda e, o=zo[b][:], g_=gb[:]: e.tensor_tensor(o, o, g_, ALU.mult))
            if not rms:
                fev = seq("pool", lambda e, o=zo[b][:], b_=bb[:]: e.tensor_tensor(o, o, b_, ALU.add))
            if silu:
                R.wait("act", fev)
                fev = seq("act", lambda e, o=zo[b][:]: e.activation(o, o, AF.Silu))
            R.wait("sync", fev)
            oev = R.dma("sync", x_dram[t * 128:(t + 1) * 128, :], zo[b][:], s_out[b])
            zo_free[b] = oev
            out_evs.append(oev)
        R.flush(final_waits=[("sync", e) for e in out_evs[-2:]])


def stage_ffn(G, xT_dram, x_dram, w1, w3, w2, z_dram, T, D, F, res_scale=ALPHA, out_scale=0.5):
    KC = D // 128
    FC = F // 128
    KH = min(16, KC)
    NH = KC // KH
    DG = D // 512
    TT = 512
    with contextlib.ExitStack() as stack:
        R = Rec(G, stack)
        xT = R.sb("ff_xT", [128, KC, TT], BF16)
        gT = R.sb("ff_gT", [128, FC, TT], BF16)
        WA = WStream(R, "ff_wa", KH * 128, nst=2, nbf=3)
        WB = WStream(R, "ff_wb", 512, nst=4, nbf=4)
        tmp = [R.sb(f"ff_tmp{i}", [128, TT], F32) for i in range(2)]
        xr = [R.sb(f"ff_xr{i}", [128, 512], F32) for i in range(4)]
        zc = [R.sb(f"ff_zc{i}", [128, 512], F32) for i in range(4)]
        psum = [R.ps(f"ff_ps{i}", [128, 512], F32) for i in range(8)]
        s_x = R.sem("ff_sx")
        s_xr = [R.sem(f"ff_sxr{i}") for i in range(4)]
        s_zo = [R.sem(f"ff_szo{i}") for i in range(4)]
        ps_free = [None] * 8
        tmp_free = [None, None]
        xr_free = [None] * 4
        zc_free = [None] * 4
        xT_free = None
        out_evs = []
        nxr = 0
        for tile in range(T // TT):
            t0 = tile * TT
            R.wait("pool", xT_free)
            ev_x = R.dma("pool", xT[:], xT_dram[:, t0:t0 + TT].rearrange("(kc p) t -> p kc t", p=128), s_x)
            R.wait("pe", ev_x)
            last_mm = None
            for f in range(FC):
                banks = []
                for wi, w in enumerate((w1, w3)):
                    pb = (f % 2) * 2 + wi
                    R.wait("pe", ps_free[pb])
                    for h in range(NH):
                        src = w[h * KH * 128:(h + 1) * KH * 128, f * 128:(f + 1) * 128].rearrange("(kc p) n -> p kc n", p=128)
                        wt, ev, slot = WA.load(src, view=lambda t_: t_[:].rearrange("p (kc n) -> p kc n", n=128))
                        wv = wt[:].rearrange("p (kc n) -> p kc n", n=128)
                        R.wait("pe", ev)
                        for k in range(KH):
                            kc = h * KH + k
                            last = (k == KH - 1)
                            mev = R.mm(psum[pb][:], wv[:, k, :], xT[:, kc, :], start=(kc == 0), stop=(kc == KC - 1), ev=last)
                        WA.release(slot, mev)
                        last_mm = mev
                    banks.append((pb, mev))
                (pa, eva), (pbk, evb) = banks
                tb = f % 2
                R.wait("act", eva)
                R.wait("act", tmp_free[tb])
                sev = R.act(tmp[tb][:], psum[pa][:], AF.Silu, ev=True)
                ps_free[pa] = sev
                R.wait("dve", sev)
                R.wait("dve", evb)
                gev = R.tt("dve", gT[:, f, :], tmp[tb][:], psum[pbk][:], ALU.mult, ev=True)
                ps_free[pbk] = gev
                tmp_free[tb] = gev
            xT_free = last_mm
            R.wait("pe", gev)
            for dg in range(DG):
                half = (dg % 2) * 4
                xr_ev = []
                for ts_ in range(4):
                    j = nxr % 4
                    nxr += 1
                    R.wait("pool", xr_free[j])
                    xr_ev.append((j, R.dma("pool", xr[j][:], x_dram[t0 + ts_ * 128:t0 + (ts_ + 1) * 128, dg * 512:(dg + 1) * 512], s_xr[j])))
                for ts_ in range(4):
                    R.wait("pe", ps_free[half + ts_])
                for f in range(FC):
                    wt, ev, slot = WB.load(w2[f * 128:(f + 1) * 128, dg * 512:(dg + 1) * 512])
                    R.wait("pe", ev)
                    for ts_ in range(4):
                        mev = R.mm(psum[half + ts_][:], gT[:, f, ts_ * 128:(ts_ + 1) * 128], wt[:], start=(f == 0), stop=(f == FC - 1), ev=(ts_ == 3))
                    WB.release(slot, mev)
                for ts_ in range(4):
                    j, xev = xr_ev[ts_]
                    R.wait("act", xev)
                    aev = R.op("act", lambda e, o=xr[j][:], c=float(res_scale): e.mul(o, o, c), ev=True)
                    R.wait("dve", aev)
                    R.wait("dve", mev)
                    R.wait("dve", zc_free[j])
                    zev = R.stt("dve", zc[j][:], psum[half + ts_][:], float(out_scale), xr[j][:], ALU.mult, ALU.add, ev=True)
                    ps_free[half + ts_] = zev
                    xr_free[j] = zev
                    R.wait("pool", zev)
                    oev = R.dma("pool", z_dram[t0 + ts_ * 128:t0 + (ts_ + 1) * 128, dg * 512:(dg + 1) * 512], zc[j][:], s_zo[j])
                    zc_free[j] = oev
                    out_evs.append(oev)
        R.flush(final_waits=[("pool", e) for e in out_evs[-4:]] + [("dve", e) for e in out_evs[-4:]])


def stage_ffn_a(G, xT_dram, w1, w3, g_dram, T, D, F):
    KC = D // 128
    FC = F // 128
    KH = min(16, KC)
    NH = KC // KH
    TT = 512
    NT = T // TT
    assert NT <= 4
    with contextlib.ExitStack() as stack:
        R = Rec(G, stack)
        xT = R.sb("fa_xT", [128, KC, T], BF16)
        WA = WStream(R, "fa_wa", KH * 128, nst=2, nbf=3)
        tmp = [R.sb(f"fa_tmp{i}", [128, TT], F32) for i in range(2)]
        go = [R.sb(f"fa_go{i}", [128, TT], BF16) for i in range(4)]
        psum = [R.ps(f"fa_ps{i}", [128, 512], F32) for i in range(8)]
        s_x = R.sem("fa_sx")
        s_go = [R.sem(f"fa_sgo{i}") for i in range(4)]
        ps_free = [None] * 8
        tmp_free = [None, None]
        go_free = [None] * 4
        ev_x = []
        for tile in range(NT):
            t0 = tile * TT
            ev_x.append(R.dma("pool", xT[:, :, t0:t0 + TT], xT_dram[:, t0:t0 + TT].rearrange("(kc p) t -> p kc t", p=128), s_x))
        out_evs = []
        ne = 0
        for f in range(FC):
            done = {}
            for wi, w in enumerate((w1, w3)):
                for h in range(NH):
                    src = w[h * KH * 128:(h + 1) * KH * 128, f * 128:(f + 1) * 128].rearrange("(kc p) n -> p kc n", p=128)
                    wt, ev, slot = WA.load(src, view=lambda t_: t_[:].rearrange("p (kc n) -> p kc n", n=128))
                    wv = wt[:].rearrange("p (kc n) -> p kc n", n=128)
                    R.wait("pe", ev)
                    for tile in range(NT):
                        pb = wi * 4 + tile
                        if f == 0 and wi == 0 and h == 0:
                            R.wait("pe", ev_x[tile])
                        if h == 0:
                            R.wait("pe", ps_free[pb])
                        for k in range(KH):
                            kc = h * KH + k
                            mev = R.mm(psum[pb][:], wv[:, k, :], xT[:, kc, tile * TT:(tile + 1) * TT],
                                       start=(kc == 0), stop=(kc == KC - 1), ev=(k == KH - 1))
                        if h == NH - 1:
                            done[(wi, tile)] = mev
                    WA.release(slot, mev)
            for tile in range(NT):
                tb = ne % 2
                gb = ne % 4
                ne += 1
                R.wait("act", done[(0, tile)])
                R.wait("act", tmp_free[tb])
                sev = R.act(tmp[tb][:], psum[tile][:], AF.Silu, ev=True)
                ps_free[tile] = sev
                R.wait("dve", sev)
                R.wait("dve", done[(1, tile)])
                R.wait("dve", go_free[gb])
                gev = R.tt("dve", go[gb][:], tmp[tb][:], psum[4 + tile][:], ALU.mult, ev=True)
                ps_free[4 + tile] = gev
                tmp_free[tb] = gev
                R.wait("pool", gev)
                oev = R.dma("pool", g_dram[f * 128:(f + 1) * 128, tile * TT:(tile + 1) * TT], go[gb][:], s_go[gb])
                go_free[gb] = oev
                out_evs.append(oev)
        R.flush(final_waits=[("pool", e) for e in out_evs[-4:]])


def stage_ffn_b(G, g_dram, x_dram, w2, z_dram, T, D, F, res_scale=ALPHA, out_scale=0.5):
    FC = F // 128
    DG = D // 512
    TT = 512
    with contextlib.ExitStack() as stack:
        R = Rec(G, stack)
        gT = R.sb("fb_gT", [128, FC, TT], BF16)
        WB = WStream(R, "fb_wb", 512, nst=8, nbf=8)
        WB.dma_eng = ("sync", "pool")
        xr = [R.sb(f"fb_xr{i}", [128, 512], F32) for i in range(4)]
        zc = [R.sb(f"fb_zc{i}", [128, 512], F32) for i in range(4)]
        psum = [R.ps(f"fb_ps{i}", [128, 512], F32) for i in range(8)]
        s_g = R.sem("fb_sg")
        s_xr = [R.sem(f"fb_sxr{i}") for i in range(4)]
        s_zo = [R.sem(f"fb_szo{i}") for i in range(4)]
        ps_free = [None] * 8
        xr_free = [None] * 4
        zc_free = [None] * 4
        gT_free = None
        out_evs = []
        nxr = 0
        ndg = 0
        FQ = (FC + 3) // 4
        for tile in range(T // TT):
            t0 = tile * TT
            R.wait("sync", gT_free)
            ev_g = None
            for f0 in range(0, FC, FQ):
                f1 = min(FC, f0 + FQ)
                ev_g = R.dma("sync", gT[:, f0:f1, :], g_dram[f0 * 128:f1 * 128, t0:t0 + TT].rearrange("(f p) t -> p f t", p=128), s_g)
            R.wait("pe", ev_g)
            for dg in range(DG):
                half = (ndg % 2) * 4
                ndg += 1
                xr_ev = []
                for ts_ in range(4):
                    j = nxr % 4
                    nxr += 1
                    R.wait("pool", xr_free[j])
                    xr_ev.append((j, R.dma("pool", xr[j][:], x_dram[t0 + ts_ * 128:t0 + (ts_ + 1) * 128, dg * 512:(dg + 1) * 512], s_xr[j])))
                for ts_ in range(4):
                    R.wait("pe", ps_free[half + ts_])
                for f in range(FC):
                    wt, ev, slot = WB.load(w2[f * 128:(f + 1) * 128, dg * 512:(dg + 1) * 512])
                    R.wait("pe", ev)
                    for ts_ in range(4):
                        mev = R.mm(psum[half + ts_][:], gT[:, f, ts_ * 128:(ts_ + 1) * 128], wt[:], start=(f == 0), stop=(f == FC - 1), ev=(ts_ == 3))
                    WB.release(slot, mev)
                for ts_ in range(4):
                    j, xev = xr_ev[ts_]
                    R.wait("act", xev)
                    aev = R.op("act", lambda e, o=xr[j][:], c=float(res_scale): e.mul(o, o, c), ev=True)
                    R.wait("dve", aev)
                    R.wait("dve", mev)
                    R.wait("dve", zc_free[j])
                    zev = R.stt("dve", zc[j][:], psum[half + ts_][:], float(out_scale), xr[j][:], ALU.mult, ALU.add, ev=True)
                    ps_free[half + ts_] = zev
                    xr_free[j] = zev
                    R.wait("pool", zev)
                    oev = R.dma("pool", z_dram[t0 + ts_ * 128:t0 + (ts_ + 1) * 128, dg * 512:(dg + 1) * 512], zc[j][:], s_zo[j])
                    zc_free[j] = oev
                    out_evs.append(oev)
            gT_free = mev
        R.flush(final_waits=[("pool", e) for e in out_evs[-4:]])


def stage_linear_tm(G, xT_dram, W, out_dram, T, K, N, res_dram=None, res_scale=1.0, out_scale=1.0):
    KC = K // 128
    TT = 512
    NG = (N + 511) // 512
    with contextlib.ExitStack() as stack:
        R = Rec(G, stack)
        xT = R.sb("lt_xT", [128, KC, TT], BF16)
        WB = WStream(R, "lt_wb", 512, nst=4, nbf=4)
        xr = [R.sb(f"lt_xr{i}", [128, 512], F32) for i in range(4)]
        zc = [R.sb(f"lt_zc{i}", [128, 512], F32) for i in range(4)]
        psum = [R.ps(f"lt_ps{i}", [128, 512], F32) for i in range(8)]
        s_x = R.sem("lt_sx")
        s_xr = [R.sem(f"lt_sxr{i}") for i in range(4)]
        s_zo = [R.sem(f"lt_szo{i}") for i in range(4)]
        ps_free = [None] * 8
        xr_free = [None] * 4
        zc_free = [None] * 4
        xT_free = None
        out_evs = []
        nxr = 0
        ngl = 0
        for tile in range(T // TT):
            t0 = tile * TT
            R.wait("pool", xT_free)
            ev_x = R.dma("pool", xT[:], xT_dram[:, t0:t0 + TT].rearrange("(kc p) t -> p kc t", p=128), s_x)
            R.wait("pe", ev_x)
            for ng in range(NG):
                n0 = ng * 512
                nw = min(512, N - n0)
                half = (ngl % 2) * 4
                ngl += 1
                slots = []
                for ts_ in range(4):
                    j = nxr % 4
                    nxr += 1
                    slots.append(j)
                xr_ev = [None] * 4
                if res_dram is not None:
                    for ts_ in range(4):
                        j = slots[ts_]
                        R.wait("pool", xr_free[j])
                        xr_ev[ts_] = R.dma("pool", xr[j][:, :nw], res_dram[t0 + ts_ * 128:t0 + (ts_ + 1) * 128, n0:n0 + nw], s_xr[j])
                for ts_ in range(4):
                    R.wait("pe", ps_free[half + ts_])
                for kc in range(KC):
                    wt, ev, slot = WB.load(W[kc * 128:(kc + 1) * 128, n0:n0 + nw], n=nw)
                    R.wait("pe", ev)
                    for ts_ in range(4):
                        mev = R.mm(psum[half + ts_][:, :nw], xT[:, kc, ts_ * 128:(ts_ + 1) * 128], wt[:, :nw], start=(kc == 0), stop=(kc == KC - 1), ev=(ts_ == 3))
                    WB.release(slot, mev)
                if tile == T // TT - 1 or True:
                    last_mm = mev
                for ts_ in range(4):
                    j = slots[ts_]
                    R.wait("dve", mev)
                    R.wait("dve", zc_free[j])
                    if res_dram is not None:
                        R.wait("act", xr_ev[ts_])
                        aev = R.op("act", lambda e, o=xr[j][:, :nw], c=float(res_scale): e.mul(o, o, c), ev=True)
                        R.wait("dve", aev)
                        zev = R.stt("dve", zc[j][:, :nw], psum[half + ts_][:, :nw], float(out_scale), xr[j][:, :nw], ALU.mult, ALU.add, ev=True)
                        xr_free[j] = zev
                    else:
                        zev = R.ts("dve", zc[j][:, :nw], psum[half + ts_][:, :nw], float(out_scale), None, ALU.mult, ev=True)
                    ps_free[half + ts_] = zev
                    R.wait("pool", zev)
                    oev = R.dma("pool", out_dram[t0 + ts_ * 128:t0 + (ts_ + 1) * 128, n0:n0 + nw], zc[j][:, :nw], s_zo[j])
                    zc_free[j] = oev
                    out_evs.append(oev)
            xT_free = last_mm
        R.flush(final_waits=[("pool", e) for e in out_evs[-4:]])


def stage_attn(G, qT_d, kT_d, v_d, cum_d, mask_d, oT_d, S, DK, nheads, scale, use_c):
    KCH = [(0, 128)] if DK == 128 else [(0, 128), (128, DK - 128)]
    NQT = S // 512
    NKB = S // 128
    with contextlib.ExitStack() as stack:
        R = Rec(G, stack)
        kT = [R.sb(f"at_kT{i}", [128, S], BF16) for i in range(len(KCH))]
        qt_ = [[R.sb(f"at_qt{b}_{i}", [128, 512], BF16) for i in range(len(KCH))] for b in range(2)]
        stq = [R.sb(f"at_stq{i}", [128, 512], F32) for i in range(2)]
        s_q = [R.sem(f"at_sq{i}") for i in range(2)]
        stq_free = [None, None]
        qt_free = [None, None]
        nsq = 0
        vv = R.sb("at_v", [128, NKB, 128], BF16)
        stg = [R.sb(f"at_stg{i}", [128, 2048], F32) for i in range(2)]
        negc = R.sb("at_negc", [128, NKB], F32)
        cbc = [R.sb(f"at_cbc{i}", [128, 512], F32) for i in range(2)]
        msk = R.sb("at_msk", [128, 4, 512], F32)
        ones = R.sb("at_ones", [128, 128], BF16)
        tmp = [R.sb(f"at_tmp{i}", [128, 512], F32) for i in range(3)]
        pT = [R.sb(f"at_pT{i}", [128, 512], BF16) for i in range(3)]
        rl = R.sb("at_rl", [128, 512], F32)
        ob = [R.sb(f"at_ob{i}", [128, 512], BF16) for i in range(2)]
        ps_s = [R.ps(f"at_pss{i}", [128, 512], F32) for i in range(3)]
        ps_o = [R.ps(f"at_pso{i}", [128, 512], F32) for i in range(2)]
        ps_l = [R.ps(f"at_psl{i}", [128, 512], F32) for i in range(2)]
        s_ld = [R.sem(f"at_sld{i}") for i in range(2)]
        s_c = R.sem("at_sc")
        s_cb = [R.sem(f"at_scb{i}") for i in range(2)]
        s_o = [R.sem(f"at_so{i}") for i in range(2)]
        last = {}

        def seq(eng, fn):
            R.wait(eng, last.get(eng))
            last[eng] = R.op(eng, fn, ev=True)
            return last[eng]

        ev_m = R.dma("sync", msk[:], mask_d.rearrange("d p t -> p d t"), s_c)
        o1 = seq("pool", lambda e: e.memset(ones[:], 1.0))
        stg_free = [None, None]
        nst = 0
        head_done = None
        out_evs = []
        cbc_free = [None, None]
        tmp_free = [None] * 3
        pT_free = [None] * 3
        pss_free = [None] * 3
        pso_free = [None, None]
        ob_free = [None, None]
        nblk = 0
        nqt = 0
        for h in range(nheads):
            R.wait("sync", head_done)
            R.wait("act", head_done)
            cast_evs = []
            for (dst_list, src) in ((kT, kT_d),):
                for ci, (k0, kn) in enumerate(KCH):
                    for c0 in range(0, S, 2048):
                        b = nst % 2
                        nst += 1
                        R.wait("sync", stg_free[b])
                        ld = R.dma("sync", stg[b][:kn, :], src[h, k0:k0 + kn, c0:c0 + 2048], s_ld[b])
                        R.wait("act", ld)
                        cev = seq("act", lambda e, o=dst_list[ci][:kn, c0:c0 + 2048], i=stg[b][:kn, :]: e.copy(o, i))
                        stg_free[b] = cev
                        cast_evs.append(cev)
            for j0 in range(0, NKB, 16):
                b = nst % 2
                nst += 1
                R.wait("sync", stg_free[b])
                ld = R.dma("sync", stg[b][:].rearrange("p (j d) -> p j d", d=128),
                           v_d[h, j0 * 128:(j0 + 16) * 128, :].rearrange("(j p) d -> p j d", p=128), s_ld[b])
                R.wait("act", ld)
                cev = seq("act", lambda e, o=vv[:, j0:j0 + 16, :], i=stg[b][:].rearrange("p (j d) -> p j d", d=128): e.copy(o, i))
                stg_free[b] = cev
                cast_evs.append(cev)
            if use_c:
                ev_nc = R.dma("sync", negc[:], cum_d[h, :].rearrange("(j p) -> p j", p=128), s_c, slow=True)
            R.wait("pe", cast_evs[-1])
            R.wait("pe", o1)
            for qt in range(NQT):
                q0 = qt * 512
                cb = nqt % 2
                ob_i = nqt % 2
                nqt += 1
                if use_c:
                    R.wait("sync", cbc_free[cb])
                    ev_cb = R.dma("sync", cbc[cb][:], cum_d[h, q0:q0 + 512].partition_broadcast(128), s_cb[cb])
                qb = (nqt - 1) % 2
                R.wait("act", qt_free[qb])
                for ci, (k0, kn) in enumerate(KCH):
                    sb_ = nsq % 2
                    nsq += 1
                    R.wait("sync", stq_free[sb_])
                    ldq = R.dma("sync", stq[sb_][:kn, :], qT_d[h, k0:k0 + kn, q0:q0 + 512], s_q[sb_])
                    R.wait("act", ldq)
                    qev = seq("act", lambda e, o=qt_[qb][ci][:kn, :], i=stq[sb_][:kn, :]: e.copy(o, i))
                    stq_free[sb_] = qev
                R.wait("pe", qev)
                R.wait("pe", pso_free[ob_i])
                nkb = 4 * qt + 4
                for j in range(nkb):
                    d = j - 4 * qt
                    c_lo = 128 * d if d > 0 else 0
                    w = 512 - c_lo
                    i3 = nblk % 3
                    nblk += 1
                    R.wait("pe", pss_free[i3])
                    for ci, (k0, kn) in enumerate(KCH):
                        sev = R.mm(ps_s[i3][:, c_lo:], kT[ci][:kn, j * 128:(j + 1) * 128], qt_[qb][ci][:kn, c_lo:512],
                                   start=(ci == 0), stop=(ci == len(KCH) - 1), ev=(ci == len(KCH) - 1))
                    R.wait("dve", sev)
                    R.wait("dve", tmp_free[i3])
                    if use_c:
                        R.wait("dve", ev_cb)
                        R.wait("dve", ev_nc)
                        tev = seq("dve", lambda e, o=tmp[i3][:, c_lo:], i=ps_s[i3][:, c_lo:], c=cbc[cb][:, c_lo:]: e.scalar_tensor_tensor(o, i, float(scale), c, ALU.mult, ALU.subtract))
                    else:
                        tev = seq("dve", lambda e, o=tmp[i3][:, c_lo:], i=ps_s[i3][:, c_lo:]: e.tensor_scalar(o, i, float(scale), None, ALU.mult))
                    pss_free[i3] = tev
                    if d >= 0:
                        R.wait("dve", ev_m)
                        tev = seq("dve", lambda e, o=tmp[i3][:, c_lo:], m_=msk[:, d, c_lo:]: e.tensor_tensor(o, o, m_, ALU.add))
                    R.wait("act", tev)
                    R.wait("act", pT_free[i3])
                    if use_c:
                        eev = seq("act", lambda e, o=pT[i3][:, c_lo:], i=tmp[i3][:, c_lo:], b_=negc[:, j:j + 1]: e.activation(o, i, AF.Exp, bias=b_))
                    else:
                        eev = seq("act", lambda e, o=pT[i3][:, c_lo:], i=tmp[i3][:, c_lo:]: e.activation(o, i, AF.Exp))
                    tmp_free[i3] = eev
                    R.wait("pe", eev)
                    R.mm(ps_o[ob_i][:, c_lo:], vv[:, j, :], pT[i3][:, c_lo:], start=(j == 0), stop=(j == nkb - 1))
                    pev = R.mm(ps_l[ob_i][:, c_lo:], ones[:], pT[i3][:, c_lo:], start=(j == 0), stop=(j == nkb - 1), ev=True)
                    pT_free[i3] = pev
                if use_c:
                    cbc_free[cb] = tev
                qt_free[qb] = sev
                R.wait("dve", pev)
                seq("dve", lambda e, i=ps_l[ob_i][:]: e.reciprocal(rl[:], i))
                R.wait("dve", ob_free[ob_i])
                nev = seq("dve", lambda e, o=ob[ob_i][:], i=ps_o[ob_i][:]: e.tensor_tensor(o, i, rl[:], ALU.mult))
                pso_free[ob_i] = nev
                R.wait("pool", nev)
                oev = R.dma("pool", oT_d[h, :, q0:q0 + 512], ob[ob_i][:], s_o[ob_i])
                ob_free[ob_i] = oev
                out_evs.append(oev)
            head_done = pev
        R.flush(final_waits=[("pool", e) for e in out_evs[-2:]])


def stage_negc(G, f_d, bf_d, scr_d, cum_d, nheads, S):
    NJ = S // 128
    with contextlib.ExitStack() as stack:
        R = Rec(G, stack)
        a = [R.sb(f"nc_a{i}", [128, NJ], F32) for i in range(2)]
        bb = R.sb("nc_b", [128, 1], F32)
        row = [R.sb(f"nc_row{i}", [1, 128], F32) for i in range(2)]
        off = R.sb("nc_off", [128, 1], F32)
        s1 = R.sem("nc_s1")
        last = {}

        def seq(eng, fn):
            R.wait(eng, last.get(eng))
            last[eng] = R.op(eng, fn, ev=True)
            return last[eng]

        def dma(eng, out, in_):
            R.wait(eng, last.get("dma"))
            for k in ("act", "dve"):
                R.wait(eng, last.get(k))
            last["dma"] = R.dma(eng, out, in_, s1)
            return last["dma"]

        def scan(bufs, n, width):
            cur = 0
            k = 1
            while k < width:
                src, dst = bufs[cur], bufs[1 - cur]
                seq("dve", lambda e, o=dst[:n, k:width], x=src[:n, k:width], y=src[:n, 0:width - k]: e.tensor_tensor(o, x, y, ALU.add))
                seq("dve", lambda e, o=dst[:n, 0:k], x=src[:n, 0:k]: e.tensor_copy(o, x))
                cur = 1 - cur
                k *= 2
            return cur

        ev = None
        for h in range(nheads):
            d1 = dma("sync", a[0][:], f_d[h, :].rearrange("(p j) -> p j", j=NJ))
            d2 = dma("sync", bb[:], bf_d[h:h + 1].partition_broadcast(128))
            R.wait("dve", d2)
            R.wait("dve", d1)
            seq("dve", lambda e: e.tensor_scalar(a[1][:], a[0][:], bb[:, 0:1], -1.0, ALU.add, ALU.mult))
            R.wait("act", last["dve"])
            seq("act", lambda e: e.activation(a[0][:], a[1][:], AF.Exp))
            R.wait("dve", last["act"])
            seq("dve", lambda e: e.tensor_scalar(a[1][:], a[0][:], 1.0, None, ALU.add))
            R.wait("act", last["dve"])
            seq("act", lambda e: e.activation(a[0][:], a[1][:], AF.Ln))
            R.wait("dve", last["act"])
            cur = scan(a, 128, NJ)
            res = a[cur]
            dma("sync", scr_d[h, 0, :].rearrange("(p o) -> p o", o=1), res[:, NJ - 1:NJ])
            d3 = dma("sync", row[0][:], scr_d[h, 0:1, :])
            R.wait("dve", d3)
            rc = scan(row, 1, 128)
            seq("dve", lambda e, o=row[1 - rc][0:1, 1:128], x=row[rc][0:1, 0:127]: e.tensor_copy(o, x))
            seq("dve", lambda e, o=row[1 - rc][0:1, 0:1]: e.memset(o, 0.0))
            dma("sync", scr_d[h, 1:2, :], row[1 - rc][:])
            d4 = dma("sync", off[:], scr_d[h, 1, :].rearrange("(p o) -> p o", o=1))
            R.wait("dve", d4)
            seq("dve", lambda e, o=a[1 - cur][:], x=res[:]: e.tensor_scalar(o, x, off[:, 0:1], None, ALU.add))
            ev = dma("sync", cum_d[h, :].rearrange("(p j) -> p j", j=NJ), a[1 - cur][:])
        R.flush(final_waits=[("sync", ev)])


POOL_WINDOWS = (2, 4, 8, 16)


def stage_pool(G, uT_d, invc_d, pw_d, ps_d, mixT_d, row0, T):
    H = 15
    with contextlib.ExitStack() as stack:
        R = Rec(G, stack)
        u = R.sb("pl_u", [128, H + T], F32)
        b0 = R.sb("pl_b0", [128, H + T], F32)
        b1 = R.sb("pl_b1", [128, H + T], F32)
        ic = R.sb("pl_ic", [128, T], F32)
        pooled = [R.sb(f"pl_p{i}", [128, T], BF16) for i in range(2)]
        wst = R.sb("pl_wst", [128, 2, 256], F32)
        wbf = R.sb("pl_wbf", [128, 2, 256], BF16)
        sc = R.sb("pl_sc", [128, 8], F32)
        yb = [R.sb(f"pl_y{i}", [128, 512], BF16) for i in range(2)]
        ps = [R.ps(f"pl_ps{i}", [128, 512], F32) for i in range(2)]
        s1 = R.sem("pl_s1")
        s_o = [R.sem(f"pl_so{i}") for i in range(2)]
        last = {}

        def seq(eng, fn):
            R.wait(eng, last.get(eng))
            last[eng] = R.op(eng, fn, ev=True)
            return last[eng]

        def dma_in(out, in_):
            for k in ("dve", "act", "pe"):
                R.wait("sync", last.get(k))
            return R.dma("sync", out, in_, s1)

        dsc = R.dma("sync", sc[:], ps_d.rearrange("(c p) -> p c", p=128), s1, slow=True)
        yb_free = [None, None]
        out_evs = []
        ny = 0
        for g, w in enumerate(POOL_WINDOWS):
            dic = dma_in(ic[:], invc_d[g, :].partition_broadcast(128))
            dw = dma_in(wst[:], pw_d[g].rearrange("(c p) d -> p c d", p=128))
            R.wait("act", dw)
            seq("act", lambda e: e.copy(wbf[:], wst[:]))
            for cc in range(2):
                ch0 = g * 256 + cc * 128
                du = dma_in(u[:], uT_d[ch0:ch0 + 128, :])
                R.wait("dve", du)
                R.wait("dve", dic)
                src = u
                k = 1
                bufs = [b0, b1]
                bi = 0
                while k < w:
                    dst = bufs[bi]
                    seq("dve", lambda e, o=dst[:, k:], x=src[:, k:], y=src[:, 0:H + T - k]: e.tensor_tensor(o, x, y, ALU.add))
                    src = dst
                    bi = 1 - bi
                    k *= 2
                dst = bufs[bi]
                seq("dve", lambda e, o=dst[:, H:], x=src[:, H:], y=ic[:]: e.tensor_tensor(o, x, y, ALU.mult))
                R.wait("dve", last.get("pe"))
                seq("dve", lambda e, o=pooled[cc][:], x=dst[:, H:], y=u[:, H:]: e.tensor_tensor(o, x, y, ALU.subtract))
            R.wait("pe", last["dve"])
            R.wait("pe", last["act"])
            for dc in range(2):
                for t0 in range(0, T, 512):
                    pb = ny % 2
                    ny += 1
                    R.wait("pe", yb_free[pb] if False else last.get("dve2_%d" % pb))
                    R.mm(ps[pb][:], wbf[:, 0, dc * 128:(dc + 1) * 128], pooled[0][:, t0:t0 + 512], start=True, stop=False)
                    last["pe"] = R.mm(ps[pb][:], wbf[:, 1, dc * 128:(dc + 1) * 128], pooled[1][:, t0:t0 + 512], start=False, stop=True, ev=True)
                    R.wait("dve", last["pe"])
                    R.wait("dve", dsc)
                    R.wait("dve", yb_free[pb])
                    col = g * 2 + dc
                    yev = seq("dve", lambda e, o=yb[pb][:], i=ps[pb][:], s_=sc[:, col:col + 1]: e.tensor_scalar(o, i, s_, None, ALU.mult))
                    last["dve2_%d" % pb] = yev
                    R.wait("pool", yev)
                    r0 = row0 + g * 256 + dc * 128
                    oev = R.dma("pool", mixT_d[r0:r0 + 128, t0:t0 + 512], yb[pb][:], s_o[pb])
                    yb_free[pb] = oev
                    out_evs.append(oev)
        R.flush(final_waits=[("pool", e) for e in out_evs[-2:]])


def stage_copy(G, dst, src):
    with contextlib.ExitStack() as stack:
        R = Rec(G, stack)
        sc = R.sem("cp_s")
        ev = R.dma("sync", dst, src, sc)
        R.flush(final_waits=[("sync", ev)])


FOX_H, HD = 24, 128
FOXW = FOX_H * HD
IN0 = 3 * FOXW + FOX_H + 1024


def _inp(nc, name, shape, dt=F32):
    return nc.dram_tensor(name, list(shape), dt, kind="ExternalInput").ap()


def _out(nc, name, shape, dt=F32):
    return nc.dram_tensor(name, list(shape), dt, kind="ExternalOutput").ap()


def _scr(nc, name, shape, dt=F32):
    return nc.dram_tensor(name, list(shape), dt).ap()


def _ffn_sub(G, nc, cur, dst, p, q, W, xT, z, ident, T, D, F):
    if getattr(G, "gs", None) is None:
        G.gs = _scr(nc, "scr_g", [F, T], BF16)
    gs = G.gs
    stage_prep(G, cur, xT, ident, T, D)
    stage_ffn_a(G, xT, W[p + "w1"], W[p + "w3"], gs, T, D, F)
    stage_ffn_b(G, gs, cur, W[p + "w2"], z, T, D, F)
    stage_ln(G, z, W[q + "g"], W[q + "b"], dst, T, D)


def _decl_ffn(nc, W, p, q, D, F):
    W[p + "w1"] = _inp(nc, p + "w1", [D, F])
    W[p + "w3"] = _inp(nc, p + "w3", [D, F])
    W[p + "w2"] = _inp(nc, p + "w2", [F, D])
    W[q + "g"] = _inp(nc, q + "g", [D])
    W[q + "b"] = _inp(nc, q + "b", [D])


def prog1(cfg):
    D, F, T = cfg["D"], cfg["F"], cfg["T"]
    nc = bass.Bass("TRN2", target_bir_lowering=False)
    x = _inp(nc, "x", [T, D])
    ident = _inp(nc, "ident", [128, 128], BF16)
    W = {}
    _decl_ffn(nc, W, "l0_ffn1_", "l0_ln_ffn1_", D, F)
    w_in = _inp(nc, "l0_w_in", [D, IN0])
    x1 = _out(nc, "x1", [T, D])
    h0 = _out(nc, "h0", [T, IN0])
    xT = _scr(nc, "scr_xT", [D, T], BF16)
    z = _scr(nc, "scr_z", [T, D])
    G = Glob(nc)
    _ffn_sub(G, nc, x, x1, "l0_ffn1_", "l0_ln_ffn1_", W, xT, z, ident, T, D, F)
    stage_prep(G, x1, xT, ident, T, D)
    stage_linear_tm(G, xT, w_in, h0, T, D, IN0)
    return nc


def prog2(cfg):
    S = cfg["S"]
    nc = bass.Bass("TRN2", target_bir_lowering=False)
    qT = _inp(nc, "qT", [3, HD, S])
    kT = _inp(nc, "kT", [3, HD, S])
    v = _inp(nc, "v", [3, S, HD])
    f = _inp(nc, "f", [3, S])
    bf_ = _inp(nc, "b_f", [3])
    mask = _inp(nc, "mask", [4, 128, 512])
    oT = _out(nc, "oT", [3, HD, S], BF16)
    cum = _scr(nc, "scr_cum", [3, S])
    scr = _scr(nc, "scr_nc", [3, 2, 128])
    G = Glob(nc)
    stage_negc(G, f, bf_, scr, cum, 3, S)
    stage_attn(G, qT, kT, v, cum, mask, oT, S, HD, 3, HD ** -0.5, True)
    return nc


def prog3_l0(cfg):
    D, F, T = cfg["D"], cfg["F"], cfg["T"]
    nc = bass.Bass("TRN2", target_bir_lowering=False)
    x1 = _inp(nc, "x1", [T, D])
    yaT = _inp(nc, "yaT", [FOXW, T], BF16)
    uT = _inp(nc, "uT", [1024, 15 + T])
    invc = _inp(nc, "invc", [4, T])
    pw = _inp(nc, "l0_pool_w", [4, 256, 256])
    psc = _inp(nc, "l0_pool_scale", [1024])
    w_out = _inp(nc, "l0_w_out", [FOXW + 1024, D])
    ident = _inp(nc, "ident", [128, 128], BF16)
    W = {}
    W["l0_ln_mix_g"] = _inp(nc, "l0_ln_mix_g", [D])
    W["l0_ln_mix_b"] = _inp(nc, "l0_ln_mix_b", [D])
    _decl_ffn(nc, W, "l0_ffn2_", "l0_ln_ffn2_", D, F)
    x3 = _out(nc, "x3", [T, D])
    mixT = _scr(nc, "scr_mixT", [FOXW + 1024, T], BF16)
    xT = _scr(nc, "scr_xT", [D, T], BF16)
    z = _scr(nc, "scr_z", [T, D])
    x2 = _scr(nc, "scr_x2", [T, D])
    G = Glob(nc)
    stage_copy(G, mixT[0:FOXW, :], yaT)
    stage_pool(G, uT, invc, pw, psc, mixT, FOXW, T)
    stage_linear_tm(G, mixT, w_out, z, T, FOXW + 1024, D, res_dram=x1, res_scale=ALPHA, out_scale=1.0)
    stage_ln(G, z, W["l0_ln_mix_g"], W["l0_ln_mix_b"], x2, T, D)
    _ffn_sub(G, nc, x2, x3, "l0_ffn2_", "l0_ln_ffn2_", W, xT, z, ident, T, D, F)
    return nc


def attn_mask():
    m = np.zeros((4, 128, 512), np.float32)
    for d in range(4):
        s = 128 * d + np.arange(128)[:, None]
        t = np.arange(512)[None, :]
        m[d] = np.where(t >= s, 0.0, -30000.0)
    return m


def run_l0(inputs, cfg, ncores=8):
    D, F, S, T = cfg["D"], cfg["F"], cfg["S"], cfg["T"]
    ident = np.eye(128, dtype=np.float32).astype(NPBF)
    x = np.ascontiguousarray(inputs["x"].reshape(S, D))
    g = lambda n: np.ascontiguousarray(np.asarray(inputs[n], np.float32))
    cores = list(range(ncores))
    maps = []
    for c in cores:
        m = {n: g(n) for n in ("l0_ffn1_w1", "l0_ffn1_w3", "l0_ffn1_w2", "l0_ln_ffn1_g", "l0_ln_ffn1_b", "l0_w_in")}
        m["x"] = x[c * T:(c + 1) * T]
        m["ident"] = ident
        maps.append(m)
    r1 = run_bass_kernel_spmd(prog1(cfg), maps, core_ids=cores).results
    x1 = np.concatenate([r["x1"] for r in r1], 0)
    h0 = np.concatenate([r["h0"] for r in r1], 0)
    q, k, v = h0[:, :FOXW], h0[:, FOXW:2 * FOXW], h0[:, 2 * FOXW:3 * FOXW]
    f = h0[:, 3 * FOXW:3 * FOXW + FOX_H]
    u = h0[:, 3 * FOXW + FOX_H:]
    hm = lambda a: np.ascontiguousarray(a.reshape(S, FOX_H, HD).transpose(1, 2, 0))
    qT, kT = hm(q), hm(k)
    vh = np.ascontiguousarray(v.reshape(S, FOX_H, HD).transpose(1, 0, 2))
    fh = np.ascontiguousarray(f.T)
    mask = attn_mask()
    maps = []
    for c in cores:
        hs = slice(3 * c, 3 * c + 3)
        maps.append(dict(qT=qT[hs], kT=kT[hs], v=vh[hs], f=fh[hs], b_f=g("l0_b_f")[hs], mask=mask))
    r2 = run_bass_kernel_spmd(prog2(cfg), maps, core_ids=cores).results
    yaT = np.concatenate([r["oT"] for r in r2], 0).reshape(FOXW, S)
    uT = np.concatenate([np.zeros((1024, 15), np.float32), u.T], 1)
    maps = []
    for c in cores:
        pos = np.arange(c * T, (c + 1) * T, dtype=np.float32)
        invc = np.stack([1.0 / np.minimum(pos + 1.0, float(w)) for w in POOL_WINDOWS]).astype(np.float32)
        m = {n: g(n) for n in ("l0_pool_w", "l0_pool_scale", "l0_w_out", "l0_ln_mix_g", "l0_ln_mix_b",
                               "l0_ffn2_w1", "l0_ffn2_w3", "l0_ffn2_w2", "l0_ln_ffn2_g", "l0_ln_ffn2_b")}
        m.update(x1=x1[c * T:(c + 1) * T], yaT=np.ascontiguousarray(yaT[:, c * T:(c + 1) * T]),
                 uT=np.ascontiguousarray(uT[:, c * T:c * T + 15 + T]), invc=invc, ident=ident)
        maps.append(m)
    r3 = run_bass_kernel_spmd(prog3_l0(cfg), maps, core_ids=cores).results
    x3 = np.concatenate([r["x3"] for r in r3], 0)
    return dict(x1=x1, h0=h0, yaT=yaT, x3=x3)


TWO_PI = 2.0 * np.pi


def stage_rope(G, src_d, dst_d, pos_d, invf_d, T, nh, stride, off):
    Wd = nh * stride
    with contextlib.ExitStack() as stack:
        R = Rec(G, stack)
        xin = [R.sb(f"rp_x{i}", [128, Wd], F32) for i in range(2)]
        xo = [R.sb(f"rp_o{i}", [128, Wd], F32) for i in range(2)]
        invf = R.sb("rp_if", [128, 32], F32)
        pi_ = R.sb("rp_pi", [128, 1], I32)
        pf = R.sb("rp_pf", [128, 1], F32)
        ang = R.sb("rp_ang", [128, 32], F32)
        r1 = R.sb("rp_r1", [128, 32], F32)
        r2 = R.sb("rp_r2", [128, 32], F32)
        aa = R.sb("rp_aa", [128, 32], F32)
        kf = R.sb("rp_kf", [128, 32], F32)
        ki = R.sb("rp_ki", [128, 32], I32)
        sn = R.sb("rp_sn", [128, 32], F32)
        cs = R.sb("rp_cs", [128, 32], F32)
        t1 = R.sb("rp_t1", [128, 32], F32)
        t2 = R.sb("rp_t2", [128, 32], F32)
        mpi = R.sb("rp_mpi", [128, 1], F32)
        s1 = R.sem("rp_s1")
        s_in = [R.sem(f"rp_sin{i}") for i in range(2)]
        s_out = [R.sem(f"rp_sout{i}") for i in range(2)]
        last = {}

        def seq(eng, fn):
            R.wait(eng, last.get(eng))
            last[eng] = R.op(eng, fn, ev=True)
            return last[eng]

        e_if = R.dma("sync", invf[:], invf_d.partition_broadcast(128), s1)
        m0 = seq("dve", lambda e: e.memset(mpi[:], -float(np.pi)))
        xin_free = [None, None]
        xo_free = [None, None]
        out_evs = []
        for t in range(T // 128):
            b = t % 2
            R.wait("sync", xin_free[b])
            R.wait("sync", last.get("dve"))
            R.wait("sync", last.get("act"))
            ld = R.dma("sync", xin[b][:], src_d[t * 128:(t + 1) * 128, :], s_in[b])
            lp = R.dma("sync", pi_[:], pos_d[t * 128:(t + 1) * 128].rearrange("(p o) -> p o", o=1), s_in[b])
            R.wait("dve", lp)
            R.wait("dve", e_if)
            seq("dve", lambda e: e.tensor_copy(pf[:], pi_[:]))
            seq("dve", lambda e: e.tensor_scalar(ang[:], invf[:], pf[:, 0:1], None, ALU.mult))
            C1 = 6.28125
            C2 = float(TWO_PI - 6.28125)
            PI_LO = 3.1415925

            def sine(dst, shift):
                seq("dve", lambda e: e.tensor_scalar(aa[:], ang[:], float(shift), None, ALU.add))
                seq("dve", lambda e: e.tensor_scalar(r1[:], aa[:], float(1.0 / TWO_PI), None, ALU.mult))
                seq("dve", lambda e: e.tensor_copy(ki[:], r1[:]))
                seq("dve", lambda e: e.tensor_copy(kf[:], ki[:]))
                seq("dve", lambda e: e.scalar_tensor_tensor(r1[:], kf[:], -C1, aa[:], ALU.mult, ALU.add))
                seq("dve", lambda e: e.scalar_tensor_tensor(r2[:], kf[:], -C2, r1[:], ALU.mult, ALU.add))
                seq("dve", lambda e: e.tensor_scalar(t1[:], r2[:], float(np.pi), -TWO_PI, ALU.is_gt, ALU.mult))
                seq("dve", lambda e: e.tensor_tensor(r1[:], r2[:], t1[:], ALU.add))
                seq("dve", lambda e: e.tensor_scalar(t1[:], r1[:], -float(np.pi), TWO_PI, ALU.is_lt, ALU.mult))
                seq("dve", lambda e: e.tensor_tensor(r2[:], r1[:], t1[:], ALU.add))
                seq("dve", lambda e: e.tensor_scalar_max(r1[:], r2[:], -PI_LO))
                seq("dve", lambda e: e.tensor_scalar_min(r2[:], r1[:], PI_LO))
                R.wait("act", last["dve"])
                seq("act", lambda e: e.activation(dst[:], r2[:], AF.Sin))
                R.wait("dve", last["act"])

            sine(sn, 0.0)
            sine(cs, 0.5 * np.pi)
            R.wait("dve", ld)
            R.wait("dve", xo_free[b])
            seq("dve", lambda e, o=xo[b][:], i=xin[b][:]: e.tensor_copy(o, i))
            for h in range(nh):
                c0 = h * stride + off
                x1 = xin[b][:, c0:c0 + 32]
                x2 = xin[b][:, c0 + 32:c0 + 64]
                seq("dve", lambda e, a=x1: e.tensor_tensor(t1[:], a, cs[:], ALU.mult))
                seq("dve", lambda e, a=x2: e.tensor_tensor(t2[:], a, sn[:], ALU.mult))
                seq("dve", lambda e, o=xo[b][:, c0:c0 + 32]: e.tensor_tensor(o, t1[:], t2[:], ALU.subtract))
                seq("dve", lambda e, a=x1: e.tensor_tensor(t1[:], a, sn[:], ALU.mult))
                seq("dve", lambda e, a=x2: e.tensor_tensor(t2[:], a, cs[:], ALU.mult))
                seq("dve", lambda e, o=xo[b][:, c0 + 32:c0 + 64]: e.tensor_tensor(o, t1[:], t2[:], ALU.add))
            xin_free[b] = last["dve"]
            R.wait("pool", last["dve"])
            oev = R.dma("pool", dst_d[t * 128:(t + 1) * 128, :], xo[b][:], s_out[b])
            xo_free[b] = oev
            out_evs.append(oev)
        R.flush(final_waits=[("pool", e) for e in out_evs[-2:]])


def stage_conv(G, aT_d, gT_d, w_d, b_d, out_d, S, taps=31):
    H = taps - 1
    CH = 2048
    with contextlib.ExitStack() as stack:
        R = Rec(G, stack)
        a = R.sb("cv_a", [128, H + CH], F32)
        g = R.sb("cv_g", [128, H + CH], F32)
        acc = [R.sb(f"cv_acc{i}", [128, CH], F32) for i in range(2)]
        w = R.sb("cv_w", [128, taps], F32)
        bb = R.sb("cv_b", [128, 1], F32)
        s1 = R.sem("cv_s1")
        s_o = [R.sem(f"cv_so{i}") for i in range(2)]
        last = {}

        def seq(eng, fn):
            R.wait(eng, last.get(eng))
            last[eng] = R.op(eng, fn, ev=True)
            return last[eng]

        R.dma("sync", w[:], w_d, s1)
        e_w = R.dma("sync", bb[:], b_d, s1)
        acc_free = [None, None]
        out_evs = []
        for ci, c0 in enumerate(range(0, S, CH)):
            b = ci % 2
            R.wait("sync", last.get("dve"))
            R.wait("sync", last.get("act"))
            R.dma("sync", a[:], aT_d[:, c0:c0 + H + CH], s1)
            ld = R.dma("sync", g[:], gT_d[:, c0:c0 + H + CH], s1)
            R.wait("act", ld)
            seq("act", lambda e: e.activation(g[:], g[:], AF.Sigmoid))
            R.wait("dve", last["act"])
            R.wait("dve", e_w)
            seq("dve", lambda e: e.tensor_tensor(a[:], a[:], g[:], ALU.mult))
            R.wait("dve", acc_free[b])
            seq("dve", lambda e, o=acc[b][:]: e.tensor_scalar(o, a[:, 0:CH], w[:, 0:1], bb[:, 0:1], ALU.mult, ALU.add))
            for k in range(1, taps):
                seq("dve", lambda e, o=acc[b][:], k=k: e.scalar_tensor_tensor(o, a[:, k:k + CH], w[:, k:k + 1], o, ALU.mult, ALU.add))
            R.wait("pool", last["dve"])
            oev = R.dma("pool", out_d[:, c0:c0 + CH], acc[b][:], s_o[b])
            acc_free[b] = oev
            out_evs.append(oev)
        R.flush(final_waits=[("pool", e) for e in out_evs[-2:]])


MLA_H = 24
IN1 = 2 * 1024 + 1024 + 512 + 64
QW = MLA_H * 192
KVW = MLA_H * 256
RMS_EPS = 1e-6


def prog3(cfg):
    D, F, T = cfg["D"], cfg["F"], cfg["T"]
    nc = bass.Bass("TRN2", target_bir_lowering=False)
    x1 = _inp(nc, "x1", [T, D])
    yaT = _inp(nc, "yaT", [FOXW, T], BF16)
    uT = _inp(nc, "uT", [1024, 15 + T])
    invc = _inp(nc, "invc", [4, T])
    pw = _inp(nc, "l0_pool_w", [4, 256, 256])
    psc = _inp(nc, "l0_pool_scale", [1024])
    w_out = _inp(nc, "l0_w_out", [FOXW + 1024, D])
    ident = _inp(nc, "ident", [128, 128], BF16)
    pos = _inp(nc, "pos", [T], I32)
    invf = _inp(nc, "invf", [32])
    W = {}
    for n in ("l0_ln_mix_g", "l0_ln_mix_b"):
        W[n] = _inp(nc, n, [D])
    _decl_ffn(nc, W, "l0_ffn2_", "l0_ln_ffn2_", D, F)
    _decl_ffn(nc, W, "l1_ffn1_", "l1_ln_ffn1_", D, F)
    w_in1 = _inp(nc, "l1_w_in", [D, IN1])
    qg = _inp(nc, "l1_q_norm_g", [1024])
    kvg = _inp(nc, "l1_kv_norm_g", [512])
    w_uq = _inp(nc, "l1_w_uq", [1024, QW])
    w_ukv = _inp(nc, "l1_w_ukv", [512, KVW])
    x4 = _out(nc, "x4", [T, D])
    h1 = _out(nc, "h1", [T, IN1])
    qr = _out(nc, "qr", [T, QW])
    kvf = _out(nc, "kvf", [T, KVW])
    kper = _out(nc, "kper", [T, 64])
    mixT = _scr(nc, "scr_mixT", [FOXW + 1024, T], BF16)
    xT = _scr(nc, "scr_xT", [D, T], BF16)
    xTq = _scr(nc, "scr_xTq", [1024, T], BF16)
    xTkv = _scr(nc, "scr_xTkv", [512, T], BF16)
    z = _scr(nc, "scr_z", [T, D])
    x2 = _scr(nc, "scr_x2", [T, D])
    x3 = _scr(nc, "scr_x3", [T, D])
    cqn = _scr(nc, "scr_cqn", [T, 1024])
    ckvn = _scr(nc, "scr_ckvn", [T, 512])
    qf = _scr(nc, "scr_qf", [T, QW])
    G = Glob(nc)
    stage_copy(G, mixT[0:FOXW, :], yaT)
    stage_pool(G, uT, invc, pw, psc, mixT, FOXW, T)
    stage_linear_tm(G, mixT, w_out, z, T, FOXW + 1024, D, res_dram=x1, res_scale=ALPHA, out_scale=1.0)
    stage_ln(G, z, W["l0_ln_mix_g"], W["l0_ln_mix_b"], x2, T, D)
    _ffn_sub(G, nc, x2, x3, "l0_ffn2_", "l0_ln_ffn2_", W, xT, z, ident, T, D, F)
    _ffn_sub(G, nc, x3, x4, "l1_ffn1_", "l1_ln_ffn1_", W, xT, z, ident, T, D, F)
    stage_prep(G, x4, xT, ident, T, D)
    stage_linear_tm(G, xT, w_in1, h1, T, D, IN1)
    stage_ln(G, h1[:, 2048:3072], qg, None, cqn, T, 1024, rms=True, eps=RMS_EPS)
    stage_prep(G, cqn, xTq, ident, T, 1024)
    stage_linear_tm(G, xTq, w_uq, qf, T, 1024, QW)
    stage_rope(G, qf, qr, pos, invf, T, MLA_H, 192, 128)
    stage_ln(G, h1[:, 3072:3584], kvg, None, ckvn, T, 512, rms=True, eps=RMS_EPS)
    stage_prep(G, ckvn, xTkv, ident, T, 512)
    stage_linear_tm(G, xTkv, w_ukv, kvf, T, 512, KVW)
    stage_rope(G, h1[:, 3584:3648], kper, pos, invf, T, 1, 64, 0)
    return nc


def prog4(cfg):
    S = cfg["S"]
    nc = bass.Bass("TRN2", target_bir_lowering=False)
    qT = _inp(nc, "qT", [3, 192, S])
    kT = _inp(nc, "kT", [3, 192, S])
    v = _inp(nc, "v", [3, S, HD])
    mask = _inp(nc, "mask", [4, 128, 512])
    aT = _inp(nc, "aT", [128, 30 + S])
    gT = _inp(nc, "gT", [128, 30 + S])
    cw = _inp(nc, "cw", [128, 31])
    cb = _inp(nc, "cb", [128, 1])
    oT = _out(nc, "oT", [3, HD, S], BF16)
    cvT = _out(nc, "cvT", [128, S])
    G = Glob(nc)
    stage_conv(G, aT, gT, cw, cb, cvT, S)
    stage_attn(G, qT, kT, v, None, mask, oT, S, 192, 3, 192 ** -0.5, False)
    return nc


def prog5(cfg):
    D, F, T = cfg["D"], cfg["F"], cfg["T"]
    nc = bass.Bass("TRN2", target_bir_lowering=False)
    x4 = _inp(nc, "x4", [T, D])
    cvt = _inp(nc, "cvt", [T, 1024])
    ydT = _inp(nc, "ydT", [MLA_H * HD, T], BF16)
    clg = _inp(nc, "l1_conv_ln_g", [1024])
    clb = _inp(nc, "l1_conv_ln_b", [1024])
    w_out = _inp(nc, "l1_w_out", [1024 + MLA_H * HD, D])
    ident = _inp(nc, "ident", [128, 128], BF16)
    W = {}
    for n in ("l1_ln_mix_g", "l1_ln_mix_b"):
        W[n] = _inp(nc, n, [D])
    _decl_ffn(nc, W, "l1_ffn2_", "l1_ln_ffn2_", D, F)
    out = _out(nc, "out", [T, D])
    mixT = _scr(nc, "scr_mixT", [1024 + MLA_H * HD, T], BF16)
    xT = _scr(nc, "scr_xT", [D, T], BF16)
    z = _scr(nc, "scr_z", [T, D])
    x5 = _scr(nc, "scr_x5", [T, D])
    yc = _scr(nc, "scr_yc", [T, 1024])
    G = Glob(nc)
    stage_ln(G, cvt, clg, clb, yc, T, 1024, silu=True)
    stage_prep(G, yc, mixT[0:1024, :], ident, T, 1024)
    stage_copy(G, mixT[1024:1024 + MLA_H * HD, :], ydT)
    stage_linear_tm(G, mixT, w_out, z, T, 1024 + MLA_H * HD, D, res_dram=x4, res_scale=ALPHA, out_scale=1.0)
    stage_ln(G, z, W["l1_ln_mix_g"], W["l1_ln_mix_b"], x5, T, D)
    _ffn_sub(G, nc, x5, out, "l1_ffn2_", "l1_ln_ffn2_", W, xT, z, ident, T, D, F)
    return nc


def run_all(inputs, cfg, ncores=8, keep=False):
    D, F, S, T = cfg["D"], cfg["F"], cfg["S"], cfg["T"]
    ident = np.eye(128, dtype=np.float32).astype(NPBF)
    x = np.ascontiguousarray(np.asarray(inputs["x"], np.float32).reshape(S, D))
    g = lambda n: np.ascontiguousarray(np.asarray(inputs[n], np.float32))
    cores = list(range(ncores))
    mask = attn_mask()
    invf = (10000.0 ** (-np.arange(32, dtype=np.float32) / np.float32(32))).astype(np.float32)
    posi = np.ascontiguousarray(np.asarray(inputs["positions"]).reshape(S).astype(np.int32))
    maps = []
    for c in cores:
        m = {n: g(n) for n in ("l0_ffn1_w1", "l0_ffn1_w3", "l0_ffn1_w2", "l0_ln_ffn1_g", "l0_ln_ffn1_b", "l0_w_in")}
        m["x"] = x[c * T:(c + 1) * T]
        m["ident"] = ident
        maps.append(m)
    r1 = run_bass_kernel_spmd(prog1(cfg), maps, core_ids=cores).results
    x1 = np.concatenate([r["x1"] for r in r1], 0)
    h0 = np.concatenate([r["h0"] for r in r1], 0)
    del r1
    hm = lambda a: np.ascontiguousarray(a.reshape(S, FOX_H, HD).transpose(1, 2, 0))
    qT, kT = hm(h0[:, :FOXW]), hm(h0[:, FOXW:2 * FOXW])
    vh = np.ascontiguousarray(h0[:, 2 * FOXW:3 * FOXW].reshape(S, FOX_H, HD).transpose(1, 0, 2))
    fh = np.ascontiguousarray(h0[:, 3 * FOXW:3 * FOXW + FOX_H].T)
    uT = np.concatenate([np.zeros((1024, 15), np.float32), h0[:, 3 * FOXW + FOX_H:].T], 1)
    del h0
    maps = []
    for c in cores:
        hs = slice(3 * c, 3 * c + 3)
        maps.append(dict(qT=qT[hs], kT=kT[hs], v=vh[hs], f=fh[hs], b_f=g("l0_b_f")[hs], mask=mask))
    r2 = run_bass_kernel_spmd(prog2(cfg), maps, core_ids=cores).results
    yaT = np.concatenate([r["oT"] for r in r2], 0).reshape(FOXW, S)
    del r2, qT, kT, vh
    names3 = ("l0_pool_w", "l0_pool_scale", "l0_w_out", "l0_ln_mix_g", "l0_ln_mix_b",
              "l0_ffn2_w1", "l0_ffn2_w3", "l0_ffn2_w2", "l0_ln_ffn2_g", "l0_ln_ffn2_b",
              "l1_ffn1_w1", "l1_ffn1_w3", "l1_ffn1_w2", "l1_ln_ffn1_g", "l1_ln_ffn1_b",
              "l1_w_in", "l1_q_norm_g", "l1_kv_norm_g", "l1_w_uq", "l1_w_ukv")
    maps = []
    for c in cores:
        p = np.arange(c * T, (c + 1) * T, dtype=np.float32)
        invc = np.stack([1.0 / np.minimum(p + 1.0, float(w)) for w in POOL_WINDOWS]).astype(np.float32)
        m = {n: g(n) for n in names3}
        m.update(x1=x1[c * T:(c + 1) * T], yaT=np.ascontiguousarray(yaT[:, c * T:(c + 1) * T]),
                 uT=np.ascontiguousarray(uT[:, c * T:c * T + 15 + T]), invc=invc, ident=ident,
                 pos=posi[c * T:(c + 1) * T], invf=invf)
        maps.append(m)
    r3 = run_bass_kernel_spmd(prog3(cfg), maps, core_ids=cores).results
    cat = lambda k: np.concatenate([r[k] for r in r3], 0)
    x4, h1, qr, kvf, kper = cat("x4"), cat("h1"), cat("qr"), cat("kvf"), cat("kper")
    del r3, x1, yaT, uT
    qT = np.ascontiguousarray(qr.reshape(S, MLA_H, 192).transpose(1, 2, 0))
    kv = kvf.reshape(S, MLA_H, 256)
    kT = np.empty((MLA_H, 192, S), np.float32)
    kT[:, :128, :] = kv[:, :, :128].transpose(1, 2, 0)
    kT[:, 128:, :] = kper.T[None, :, :]
    vh = np.ascontiguousarray(kv[:, :, 128:].transpose(1, 0, 2))
    aT = np.concatenate([np.zeros((1024, 30), np.float32), h1[:, :1024].T], 1)
    gT = np.concatenate([np.zeros((1024, 30), np.float32), h1[:, 1024:2048].T], 1)
    cw = np.ascontiguousarray(g("l1_conv_w")[:, 0, :].T)
    cb = g("l1_conv_b").reshape(1024, 1)
    maps = []
    for c in cores:
        hs = slice(3 * c, 3 * c + 3)
        cs_ = slice(128 * c, 128 * (c + 1))
        maps.append(dict(qT=qT[hs], kT=kT[hs], v=vh[hs], mask=mask, aT=np.ascontiguousarray(aT[cs_]),
                         gT=np.ascontiguousarray(gT[cs_]), cw=np.ascontiguousarray(cw[cs_]), cb=np.ascontiguousarray(cb[cs_])))
    r4 = run_bass_kernel_spmd(prog4(cfg), maps, core_ids=cores).results
    ydT = np.concatenate([r["oT"] for r in r4], 0).reshape(MLA_H * HD, S)
    cvt = np.ascontiguousarray(np.concatenate([r["cvT"] for r in r4], 0).T)
    del r4, qT, kT, vh, aT, gT
    names5 = ("l1_conv_ln_g", "l1_conv_ln_b", "l1_w_out", "l1_ln_mix_g", "l1_ln_mix_b",
              "l1_ffn2_w1", "l1_ffn2_w3", "l1_ffn2_w2", "l1_ln_ffn2_g", "l1_ln_ffn2_b")
    maps = []
    for c in cores:
        m = {n: g(n) for n in names5}
        m.update(x4=x4[c * T:(c + 1) * T], cvt=cvt[c * T:(c + 1) * T],
                 ydT=np.ascontiguousarray(ydT[:, c * T:(c + 1) * T]), ident=ident)
        maps.append(m)
    r5 = run_bass_kernel_spmd(prog5(cfg), maps, core_ids=cores).results
    out = np.concatenate([r["out"] for r in r5], 0)
    if keep:
        return dict(out=out, x4=x4, h1=h1, qr=qr, kvf=kvf, kper=kper, ydT=ydT, cvt=cvt)
    return out


CFG_FULL = dict(D=4096, F=11008, S=16384, T=2048)


def kernel(**inputs):
    out = run_all(inputs, CFG_FULL, ncores=8)
    return np.ascontiguousarray(out.reshape(1, CFG_FULL["S"], CFG_FULL["D"]).astype(np.float32))
```
